# Optimizing a Trainium2 kernel written in Bass

```python
import math
import jax, jax.numpy as jnp
from jax import lax
import numpy as np

D_MODEL = 2048
BATCH = 4
SEQ = 4096
DEPTH = 1

N_META = 16
GLA_HEADS = 4
GLA_DK = 256
GLA_DV = 512
GLA_LOWRANK = 16
GLA_TAU = 16.0
GLA_CHUNK = 64
DIFF_HEADS = 8
DIFF_HD = 128
Q_BLOCK = 128
D_FF = 5504
EPS = 1e-6

GLA_QK_W = GLA_HEADS * GLA_DK
GLA_V_W = GLA_HEADS * GLA_DV
DIFF_QK_W = DIFF_HEADS * 2 * DIFF_HD
DIFF_V_W = DIFF_HEADS * 2 * DIFF_HD
IN_SIZES = [GLA_QK_W, GLA_QK_W, GLA_V_W, GLA_V_W, GLA_LOWRANK, GLA_LOWRANK,
            DIFF_QK_W, DIFF_QK_W, DIFF_V_W, D_MODEL, D_MODEL]
IN_W = int(sum(IN_SIZES))
IN_SPLITS = [int(c) for c in np.cumsum(IN_SIZES)[:-1]]

kernel_name = "hybrid_gla_diffattn_macaron_block"


def rms_norm(x, g):
    xf = x.astype(jnp.float32)
    y = xf * lax.rsqrt(jnp.mean(xf * xf, axis=-1, keepdims=True) + EPS)
    return (y * g.astype(jnp.float32)).astype(x.dtype)


def swiglu(x, w_gate, w_up, w_down):
    return (jax.nn.silu(x @ w_gate) * (x @ w_up)) @ w_down


def alibi_slopes(n_heads):
    return jnp.asarray(np.array([2.0 ** (-8.0 * (i + 1) / n_heads) for i in range(n_heads)], np.float32))


def diff_lambda_init(layer_idx):
    return 0.8 - 0.6 * math.exp(-0.3 * layer_idx)


def gla_chunked(q, k, v, log_a, strict):
    B, H, P, dk = q.shape
    dv = v.shape[-1]
    n = P // GLA_CHUNK
    C = GLA_CHUNK
    q = q.reshape(B, H, n, C, dk)
    k = k.reshape(B, H, n, C, dk)
    v = v.reshape(B, H, n, C, dv)
    b = jnp.cumsum(log_a.reshape(B, H, n, C, dk), axis=3)
    b_last = b[:, :, :, -1:, :]
    q_dec = q * jnp.exp(b)
    k_inv = k * jnp.exp(-b)
    k_end = k * jnp.exp(b_last - b)
    mask = jnp.tril(jnp.ones((C, C), dtype=bool), k=-1 if strict else 0)
    scores = jnp.einsum('bhnik,bhnjk->bhnij', q_dec, k_inv)
    o_intra = jnp.einsum('bhnij,bhnjv->bhniv', jnp.where(mask, scores, 0.0), v)

    def step(state, xs):
        qd, ke, vc, dl = xs
        o = jnp.einsum('bhck,bhkv->bhcv', qd, state)
        state = state * dl[..., None] + jnp.einsum('bhck,bhcv->bhkv', ke, vc)
        return state, o

    xs = (jnp.moveaxis(q_dec, 2, 0), jnp.moveaxis(k_end, 2, 0), jnp.moveaxis(v, 2, 0),
          jnp.moveaxis(jnp.exp(b_last[:, :, :, 0, :]), 2, 0))
    state0 = jnp.zeros((B, H, dk, dv), jnp.float32)
    _, o_inter = lax.scan(step, state0, xs)
    o = o_intra + jnp.moveaxis(o_inter, 0, 2)
    return o.reshape(B, H, P, dv)


def gla_branch(p_q, p_k, p_v, p_r, p_af, p_ab, wa2_f, ba_f, wa2_b, ba_b, norm_w):
    B, L, _ = p_q.shape
    in_dtype = p_q.dtype

    def heads(t, d):
        return t.reshape(B, L, GLA_HEADS, d).transpose(0, 2, 1, 3).astype(jnp.float32)

    q = heads(p_q, GLA_DK) * (GLA_DK ** -0.5)
    k = heads(p_k, GLA_DK)
    v = heads(p_v, GLA_DV)
    log_af = heads(jax.nn.log_sigmoid((p_af @ wa2_f + ba_f).astype(jnp.float32)) / GLA_TAU, GLA_DK)
    log_ab = heads(jax.nn.log_sigmoid((p_ab @ wa2_b + ba_b).astype(jnp.float32)) / GLA_TAU, GLA_DK)
    pad = (-N_META) % GLA_CHUNK

    def padt(t):
        return jnp.pad(t, ((0, 0), (0, 0), (pad, 0), (0, 0)))

    def flip(t):
        return jnp.flip(t, axis=2)

    q, k, v, log_af, log_ab = padt(q), padt(k), padt(v), padt(log_af), padt(log_ab)
    o_fwd = gla_chunked(q, k, v, log_af, strict=False)
    o_bwd = flip(gla_chunked(flip(q), flip(k), flip(v), flip(log_ab), strict=True))
    o = (o_fwd + o_bwd)[:, :, pad:]
    o = rms_norm(o, norm_w)
    o = o.transpose(0, 2, 1, 3).reshape(B, L, GLA_V_W).astype(in_dtype)
    return o * jax.nn.silu(p_r)


def diff_branch(p_q, p_k, p_v, lq1, lk1, lq2, lk2, norm_w, lambda_init):
    B, L, _ = p_q.shape
    q = p_q.reshape(B, L, DIFF_HEADS, 2, DIFF_HD).transpose(3, 0, 2, 1, 4) * (DIFF_HD ** -0.5)
    k = p_k.reshape(B, L, DIFF_HEADS, 2, DIFF_HD).transpose(3, 0, 2, 1, 4)
    v = p_v.reshape(B, L, DIFF_HEADS, 2 * DIFF_HD).transpose(0, 2, 1, 3)
    lam = (jnp.exp(jnp.sum(lq1 * lk1)) - jnp.exp(jnp.sum(lq2 * lk2)) + lambda_init).astype(jnp.float32)
    pos = jnp.arange(L)
    slopes = alibi_slopes(DIFF_HEADS)
    k1, k2 = k[0], k[1]

    def attend(q1b, q2b, pos_b):
        dist = jnp.abs(pos_b[:, None] - pos[None, :]).astype(jnp.float32)
        bias = -slopes[:, None, None] * dist
        a1 = jax.nn.softmax(jnp.einsum('bhqd,bhkd->bhqk', q1b, k1).astype(jnp.float32) + bias, axis=-1)
        a2 = jax.nn.softmax(jnp.einsum('bhqd,bhkd->bhqk', q2b, k2).astype(jnp.float32) + bias, axis=-1)
        w = a1 - lam * a2
        return jnp.einsum('bhqk,bhkv->bhqv', w.astype(v.dtype), v)

    o_meta = attend(q[0][:, :, :N_META], q[1][:, :, :N_META], pos[:N_META])
    n_blocks = (L - N_META) // Q_BLOCK

    def blocks(t):
        return t[:, :, N_META:].reshape(B, DIFF_HEADS, n_blocks, Q_BLOCK, DIFF_HD).transpose(2, 0, 1, 3, 4)

    o_real = lax.map(lambda a: attend(a[0], a[1], a[2]),
                     (blocks(q[0]), blocks(q[1]), pos[N_META:].reshape(n_blocks, Q_BLOCK)))
    o_real = o_real.transpose(1, 2, 0, 3, 4).reshape(B, DIFF_HEADS, L - N_META, 2 * DIFF_HD)
    o = jnp.concatenate([o_meta, o_real], axis=2)
    o = rms_norm(o, norm_w) * (1.0 - lambda_init)
    return o.transpose(0, 2, 1, 3).reshape(B, L, DIFF_V_W)


def setup_inputs(seed: int = 0) -> dict:
    key = jax.random.key(seed)
    ks = jax.random.split(key, 32)
    f32 = jnp.float32

    def nrm(k, shape, scale):
        return jax.random.normal(k, shape, f32) * scale

    def gain(k, shape):
        return 1.0 + 0.02 * jax.random.normal(k, shape, f32)

    dm = D_MODEL ** -0.5
    return {
        "x": nrm(ks[0], (BATCH, SEQ, D_MODEL), 1.0),
        "meta_tokens": nrm(ks[1], (N_META, D_MODEL), 1.0),
        "ffn1_norm": gain(ks[2], (DEPTH, D_MODEL)),
        "ffn1_w_gate": nrm(ks[3], (DEPTH, D_MODEL, D_FF), dm),
        "ffn1_w_up": nrm(ks[4], (DEPTH, D_MODEL, D_FF), dm),
        "ffn1_w_down": nrm(ks[5], (DEPTH, D_FF, D_MODEL), D_FF ** -0.5),
        "mix_norm": gain(ks[6], (DEPTH, D_MODEL)),
        "w_in": nrm(ks[7], (DEPTH, D_MODEL, IN_W), dm),
        "gla_wa2_fwd": nrm(ks[8], (DEPTH, GLA_LOWRANK, GLA_QK_W), GLA_LOWRANK ** -0.5),
        "gla_ba_fwd": nrm(ks[9], (DEPTH, GLA_QK_W), 0.1),
        "gla_wa2_bwd": nrm(ks[10], (DEPTH, GLA_LOWRANK, GLA_QK_W), GLA_LOWRANK ** -0.5),
        "gla_ba_bwd": nrm(ks[11], (DEPTH, GLA_QK_W), 0.1),
        "gla_out_norm": gain(ks[12], (DEPTH, GLA_DV)),
        "diff_lambda_q1": nrm(ks[13], (DEPTH, DIFF_HD), 0.1),
        "diff_lambda_k1": nrm(ks[14], (DEPTH, DIFF_HD), 0.1),
        "diff_lambda_q2": nrm(ks[15], (DEPTH, DIFF_HD), 0.1),
        "diff_lambda_k2": nrm(ks[16], (DEPTH, DIFF_HD), 0.1),
        "diff_out_norm": gain(ks[17], (DEPTH, 2 * DIFF_HD)),
        "w_branch_gla": nrm(ks[18], (DEPTH, GLA_V_W, D_MODEL), GLA_V_W ** -0.5),
        "w_branch_diff": nrm(ks[19], (DEPTH, DIFF_V_W, D_MODEL), DIFF_V_W ** -0.5),
        "w_out": nrm(ks[20], (DEPTH, D_MODEL, D_MODEL), dm),
        "ffn2_norm": gain(ks[21], (DEPTH, D_MODEL)),
        "ffn2_w_gate": nrm(ks[22], (DEPTH, D_MODEL, D_FF), dm),
        "ffn2_w_up": nrm(ks[23], (DEPTH, D_MODEL, D_FF), dm),
        "ffn2_w_down": nrm(ks[24], (DEPTH, D_FF, D_MODEL), D_FF ** -0.5),
        "final_norm": gain(ks[25], (D_MODEL,)),
    }


def reference(x, meta_tokens, ffn1_norm, ffn1_w_gate, ffn1_w_up, ffn1_w_down, mix_norm, w_in,
              gla_wa2_fwd, gla_ba_fwd, gla_wa2_bwd, gla_ba_bwd, gla_out_norm,
              diff_lambda_q1, diff_lambda_k1, diff_lambda_q2, diff_lambda_k2, diff_out_norm,
              w_branch_gla, w_branch_diff, w_out, ffn2_norm, ffn2_w_gate, ffn2_w_up, ffn2_w_down,
              final_norm):
    B = x.shape[0]
    meta = jnp.broadcast_to(meta_tokens[None].astype(x.dtype), (B, N_META, x.shape[-1]))
    h = jnp.concatenate([meta, x], axis=1)
    for l in range(DEPTH):
        h = h + 0.5 * swiglu(rms_norm(h, ffn1_norm[l]), ffn1_w_gate[l], ffn1_w_up[l], ffn1_w_down[l])
        u = rms_norm(h, mix_norm[l])
        p = u @ w_in[l]
        g_q, g_k, g_v, g_r, g_af, g_ab, d_q, d_k, d_v, gate_a, gate_b = jnp.split(p, IN_SPLITS, axis=-1)
        y_gla = gla_branch(g_q, g_k, g_v, g_r, g_af, g_ab, gla_wa2_fwd[l], gla_ba_fwd[l],
                           gla_wa2_bwd[l], gla_ba_bwd[l], gla_out_norm[l])
        y_diff = diff_branch(d_q, d_k, d_v, diff_lambda_q1[l], diff_lambda_k1[l], diff_lambda_q2[l],
                             diff_lambda_k2[l], diff_out_norm[l], diff_lambda_init(l))
        merged = (jax.nn.sigmoid(gate_a) * (y_gla @ w_branch_gla[l])
                  + jax.nn.sigmoid(gate_b) * (y_diff @ w_branch_diff[l]))
        h = h + merged @ w_out[l]
        h = h + 0.5 * swiglu(rms_norm(h, ffn2_norm[l]), ffn2_w_gate[l], ffn2_w_up[l], ffn2_w_down[l])
    return rms_norm(h, final_norm)[:, N_META:]
```

```python
import contextlib
import numpy as np
import ml_dtypes
import concourse.bass as bass
import concourse.mybir as mybir
from concourse.bass_utils import run_bass_kernel_spmd

F32 = mybir.dt.float32
BF16 = mybir.dt.bfloat16
AF = mybir.ActivationFunctionType
ALU = mybir.AluOpType
AX = mybir.AxisListType

ENGS = ["pe", "act", "dve", "pool", "sp"]
_PAR = {}

D = 2048
FF = 5504
NFC = 43
KC = 16
NTOK = 2048
NMETA = 16
LTOT = 4112
EPS = 1e-6
C0 = 3968
TW = 8080
GROUPS = [[0, 1], [2, 3], [4, 5], [6, 7]]


class T:
    __slots__ = ("name", "writer", "readers", "dram", "wd", "ap")

    def __init__(self, name, ap=None, dram=False):
        self.name = name
        self.writer = None
        self.readers = []
        self.dram = dram
        self.wd = []
        self.ap = ap


class Op:
    __slots__ = ("eng", "fn", "deps", "kind", "idx", "needed", "dsem", "dval", "seq")

    def __init__(self, eng, fn, kind):
        self.eng = eng
        self.fn = fn
        self.kind = kind
        self.deps = []
        self.idx = 0
        self.needed = False
        self.dsem = None
        self.dval = 0


class Sched:
    def __init__(self, nc):
        self.nc = nc
        self.ops = {e: [] for e in ENGS}
        self.lastc = {e: None for e in ENGS}
        self.pending_d = []
        self.nseq = 0

    def _rec(self, o, reads, writes):
        deps = []
        rw = set()
        for t in reads:
            if t.dram:
                deps.extend(t.wd)
            elif t.writer is not None:
                deps.append(t.writer)
                rw.add(id(t.writer))
        for t in writes:
            if t.dram:
                continue
            if t.writer is not None:
                deps.append(t.writer)
            deps.extend(t.readers)
        out = []
        seen = set()
        for d in deps:
            if id(d) in seen:
                continue
            seen.add(id(d))
            if d.kind == "c" and d.eng == o.eng and o.kind == "c":
                if o.eng == "pe":
                    continue
            out.append(d)
            d.needed = True
        o.deps = out
        for t in reads:
            if not t.dram:
                t.readers.append(o)
        for t in writes:
            if t.dram:
                t.wd.append(o)
            else:
                t.writer = o
                t.readers = []
        self.ops[o.eng].append(o)
        o.seq = self.nseq
        self.nseq += 1
        if o.kind == "c":
            self.lastc[o.eng] = o
        else:
            self.pending_d.append(o)
        return o

    def c(self, eng, fn, reads=(), writes=()):
        return self._rec(Op(eng, fn, "c"), reads, writes)

    def dma(self, q, out_ap, in_ap, reads=(), writes=()):
        return self._rec(Op(q, lambda h: h.dma_start(out=out_ap, in_=in_ap), "d"), reads, writes)

    def dmaf(self, q, fn, reads=(), writes=()):
        return self._rec(Op(q, fn, "d"), reads, writes)

    def coll(self, fn, reads=(), writes=()):
        return self._rec(Op("pool", fn, "x"), reads, writes)

    def barrier(self):
        lasts = [self.lastc[e] for e in ENGS if self.lastc[e] is not None]
        pend = list(self.pending_d)
        self.pending_d = []
        for e in ENGS:
            o = Op(e, None, "n")
            o.deps = [d for d in lasts if d.eng != e] + pend
            for d in o.deps:
                d.needed = True
            self.ops[e].append(o)

    def emit(self, nd=80):
        nc = self.nc
        for e in ENGS:
            i = 0
            for o in self.ops[e]:
                if o.kind == "c" and o.needed:
                    i += 1
                    o.idx = i
        cnt = [0] * nd
        k = 0
        nx = 0
        alld = sorted([o for e in ENGS for o in self.ops[e] if o.kind in "dx"], key=lambda o: o.seq)
        nsw = 28
        ksw = 0
        for _ in range(1):
            for o in alld:
                if o.kind == "d":
                    if o.eng == "pool":
                        s = ksw % nsw
                        ksw += 1
                    else:
                        s = nsw + k % (nd - nsw)
                        k += 1
                    cnt[s] += 16
                    o.dsem = s
                    o.dval = cnt[s]
                elif o.kind == "x":
                    o.dsem = nx
                    o.dval = 1
                    nx += 1
        with contextlib.ExitStack() as st:
            esem = {e: st.enter_context(nc.semaphore("s_" + e)) for e in ENGS}
            dsem = [st.enter_context(nc.semaphore("d%d" % i)) for i in range(nd)]
            xsem = [st.enter_context(nc.semaphore("x%d" % i)) for i in range(nx)]
            block = st.enter_context(nc.Block())

            def run(e, h):
                seen = {}
                for o in self.ops[e]:
                    w = {}
                    for d in o.deps:
                        if d.kind == "c":
                            key = ("e", d.eng)
                            v = d.idx
                        elif d.kind == "d":
                            key = ("d", d.dsem)
                            v = d.dval
                        else:
                            key = ("x", d.dsem)
                            v = 1
                        if w.get(key, 0) < v:
                            w[key] = v
                    if o.kind == "d" and o.dval > 16:
                        key = ("d", o.dsem)
                        if w.get(key, 0) < o.dval - 16:
                            w[key] = o.dval - 16
                    for key, v in w.items():
                        if seen.get(key, 0) >= v:
                            continue
                        seen[key] = v
                        sem = esem[key[1]] if key[0] == "e" else (dsem[key[1]] if key[0] == "d" else xsem[key[1]])
                        h.wait_ge(sem, v)
                    if o.kind == "n":
                        continue
                    ins = o.fn(h)
                    if o.kind == "c":
                        if o.needed:
                            ins.then_inc(esem[e], 1)
                    elif o.kind == "d":
                        ins.then_inc(dsem[o.dsem], 16)
                    else:
                        ins.then_inc(xsem[o.dsem], 1)

            @block.tensor
            def _(h):
                run("pe", h)

            @block.scalar
            def _(h):
                run("act", h)

            @block.vector
            def _(h):
                run("dve", h)

            @block.gpsimd
            def _(h):
                run("pool", h)

            @block.sync
            def _(h):
                par = h.partition_id() % 2
                _PAR["p"] = par
                for q_ in range(4):
                    _PAR[q_] = h.snap(par * 4 + q_)
                run("sp", h)


class KB:
    def __init__(self, nc, S):
        self.nc = nc
        self.S = S
        self.AW = 51200
        self.arena = nc.alloc_sbuf_tensor("arena", [128, self.AW], F32).ap()
        self.off = 0
        banks = [nc.alloc_psum_tensor("ps%d" % i, [128, 512], F32).ap() for i in range(8)]
        self.ps = [T("ps%d" % i, b) for i, b in enumerate(banks)]
        self.psc = {}
        self.uid = 0

    def alloc(self, name, nbytes, dt=F32):
        words = (nbytes + 3) // 4
        words = (words + 7) // 8 * 8
        assert self.off + words <= self.AW, (name, self.off, words)
        v = self.arena[:, self.off:self.off + words]
        self.off += words
        if dt != F32:
            v = v.bitcast(dt)[:, 0:nbytes // 2]
        else:
            v = v[:, 0:nbytes // 4]
        self.uid += 1
        return T("%s_%d" % (name, self.uid), v)

    def ring(self, name, n, nbytes, dt=F32):
        return [self.alloc(name + str(i), nbytes, dt) for i in range(n)]

    def nextps(self, lo=0, hi=8):
        k = self.psc.get((lo, hi), 0)
        self.psc[(lo, hi)] = k + 1
        return self.ps[lo + k % (hi - lo)]


def v3(ap, a):
    return ap.rearrange("p (a b) -> p a b", a=a)


def norm_transpose(K, src, subs, gain, gcol, dstT, ident, xt_ring, xs_ring, small_r, junk, ctr, psr=(0, 8)):
    S = K.S
    W = dstT.ap.shape[1] // KC
    dT = v3(dstT.ap, KC)
    for (r0, n, toff) in subs:
        xt = xt_ring[ctr[0] % len(xt_ring)]
        xs = xs_ring[ctr[0] % len(xs_ring)]
        col = ctr[0] % 8
        ctr[0] += 1
        S.dma("sp", xt.ap[:n, :], src.ap[r0:r0 + n, :], reads=[src], writes=[xt])
        small = small_r[col]
        ss = small.ap[:n, 0:1]
        rs = small.ap[:n, 1:2]
        S.c("dve", lambda h, ss=ss: h.memset(ss, 0.0), writes=[small])
        S.c("act", lambda h, xt=xt, xs=xs, n=n, ss=ss: h.activation(xs.ap[:n, :], xt.ap[:n, :], AF.Square, accum_out=ss),
            reads=[xt, small], writes=[xs, small])
        S.c("dve", lambda h, ss=ss, rs=rs: h.tensor_scalar(rs, ss, 1.0 / D, EPS, ALU.mult, ALU.add), reads=[small], writes=[small])
        S.c("act", lambda h, rs=rs: h.activation(rs, rs, AF.Sqrt), reads=[small], writes=[small])
        S.c("dve", lambda h, rs=rs: h.reciprocal(rs, rs), reads=[small], writes=[small])
        S.c("dve", lambda h, xs=xs, xt=xt, n=n, rs=rs: h.tensor_scalar(xs.ap[:n, :], xt.ap[:n, :], rs, None, ALU.mult),
            reads=[xt, small], writes=[xs])
        for half in range(2):
            ps = K.nextps(*psr)
            psb = ps.ap.bitcast(BF16)
            for cc in range(8):
                c = half * 8 + cc
                S.c("pe", lambda h, psb=psb, cc=cc, xs=xs, n=n, c=c: h.transpose(psb[:, cc * 128:cc * 128 + n], xs.ap[:n, c * 128:(c + 1) * 128], ident.ap[:n, :n]),
                    reads=[xs, ident], writes=[ps])
            pv = v3(psb, 8)
            gv = gain.ap[:, gcol + half * 8:gcol + half * 8 + 8].unsqueeze(2).to_broadcast([128, 8, n])
            S.c("dve", lambda h, dT=dT, half=half, toff=toff, n=n, pv=pv, gv=gv: h.tensor_tensor(dT[:, half * 8:half * 8 + 8, toff:toff + n], pv[:, :, :n], gv, ALU.mult),
                reads=[ps, gain], writes=[dstT])


def ffn(K, src, dst, tiles, Wg, Wu, Wd, gain, gcol, ident, post):
    S = K.S
    mark = K.off
    TT = 1024
    xnT = K.alloc("xnT", KC * TT * 2, BF16)
    actT = K.alloc("actT", NFC * TT * 2, BF16)
    wg_r = K.ring("wg", 3, KC * 128 * 2, BF16)
    wu_r = K.ring("wu", 3, KC * 128 * 2, BF16)
    wd_r = K.ring("wd", 3, 4 * 512 * 2, BF16)
    xt_r = K.ring("xt", 2, D * 4)
    xs_r = K.ring("xs", 2, D * 2, BF16)
    junk = None
    small = K.ring("small", 8, 32)
    sg_r = K.ring("sg", 3, 512 * 4)
    xb_r = K.ring("xb", 4, 512 * 4)
    hb_r = K.ring("hb", 2, 512 * 4)
    ctr = [0]
    wgv = Wg.ap.rearrange("(kc p) f -> p kc f", p=128)
    wuv = Wu.ap.rearrange("(kc p) f -> p kc f", p=128)
    wdv = Wd.ap.rearrange("(c p) d -> p c d", p=128)
    aT = v3(actT.ap, NFC)
    xT = v3(xnT.ap, KC)
    wi = 0
    di = 0
    ei = 0
    xi = [0]
    for subs0 in tiles:
        subs = []
        toff = 0
        for (r0, n) in subs0:
            subs.append((r0, n, toff))
            toff += n
        NT = toff
        norm_transpose(K, src, subs, gain, gcol, xnT, ident, xt_r, xs_r, small, junk, ctr)
        sts = [(s0, min(512, NT - s0)) for s0 in range(0, NT, 512)]
        for blk in range(NFC):
            wg = wg_r[wi % 3]
            wu = wu_r[wi % 3]
            wi += 1
            S.dma("pool", v3(wg.ap, KC), wgv[:, :, blk * 128:(blk + 1) * 128], reads=[Wg], writes=[wg])
            S.dma("pool", v3(wu.ap, KC), wuv[:, :, blk * 128:(blk + 1) * 128], reads=[Wu], writes=[wu])
            wg3 = v3(wg.ap, KC)
            wu3 = v3(wu.ap, KC)
            for (s0, nn) in sts:
                pg = K.nextps()
                pu = K.nextps()
                for kc in range(KC):
                    S.c("pe", lambda h, pg=pg, wg3=wg3, kc=kc, s0=s0, nn=nn: h.matmul(pg.ap[:, :nn], wg3[:, kc, :], xT[:, kc, s0:s0 + nn], start=(kc == 0), stop=(kc == KC - 1)),
                        reads=[wg, xnT], writes=[pg])
                for kc in range(KC):
                    S.c("pe", lambda h, pu=pu, wu3=wu3, kc=kc, s0=s0, nn=nn: h.matmul(pu.ap[:, :nn], wu3[:, kc, :], xT[:, kc, s0:s0 + nn], start=(kc == 0), stop=(kc == KC - 1)),
                        reads=[wu, xnT], writes=[pu])
                sg = sg_r[ei % 3]
                ei += 1
                S.c("act", lambda h, sg=sg, pg=pg, nn=nn: h.activation(sg.ap[:, :nn], pg.ap[:, :nn], AF.Silu), reads=[pg], writes=[sg])
                S.c("dve", lambda h, blk=blk, s0=s0, nn=nn, sg=sg, pu=pu: h.tensor_tensor(aT[:, blk, s0:s0 + nn], sg.ap[:, :nn], pu.ap[:, :nn], ALU.mult),
                    reads=[sg, pu], writes=[actT])
        for db in range(4):
            accs = [K.nextps() for _ in subs]
            for cg in range(0, NFC, 4):
                ncg = min(4, NFC - cg)
                wd = wd_r[di % 3]
                di += 1
                wd3 = v3(wd.ap, 4)
                S.dma("pool", wd3[:, :ncg, :], wdv[:, cg:cg + ncg, db * 512:(db + 1) * 512], reads=[Wd], writes=[wd])
                for i in range(ncg):
                    c = cg + i
                    for si, (r0, n, toff) in enumerate(subs):
                        S.c("pe", lambda h, acc=accs[si], c=c, toff=toff, n=n, wd3=wd3, i=i: h.matmul(acc.ap[:n, :], aT[:, c, toff:toff + n], wd3[:, i, :], start=(c == 0), stop=(c == NFC - 1)),
                            reads=[actT, wd], writes=[accs[si]])
            xbs = {}
            def ld(si):
                r0, n, toff = subs[si]
                xb = xb_r[(xi[0] + si) % 4]
                S.dma("sp", xb.ap[:n, :], src.ap[r0:r0 + n, db * 512:(db + 1) * 512], reads=[src], writes=[xb])
                xbs[si] = xb
            for si in range(min(3, len(subs))):
                ld(si)
            for si, (r0, n, toff) in enumerate(subs):
                if si + 3 < len(subs):
                    ld(si + 3)
                xb = xbs[si]
                hb = hb_r[ei % 2]
                ei += 1
                S.c("dve", lambda h, hb=hb, acc=accs[si], xb=xb, n=n: h.scalar_tensor_tensor(out=hb.ap[:n, :], in0=acc.ap[:n, :], scalar=0.5, in1=xb.ap[:n, :], op0=ALU.mult, op1=ALU.add),
                    reads=[accs[si], xb], writes=[hb])
                S.dma("act", dst.ap[r0:r0 + n, db * 512:(db + 1) * 512], hb.ap[:n, :], reads=[hb], writes=[dst])
            xi[0] += len(subs)
        post(subs, xnT, xt_r, xs_r, small, junk, ctr)
    S.barrier()
    K.off = mark


def ffn2(K, src, dst, tiles, Wg, Wu, Wd, gain, gcol, ident, post_setup, post_sub):
    S = K.S
    mark = K.off
    TT = 1024
    xnT = K.alloc("xnT", KC * TT * 2, BF16)
    actT = K.alloc("actT", NFC * TT * 2, BF16)
    wg_r = K.ring("wg", 3, KC * 128 * 2, BF16)
    wu_r = K.ring("wu", 3, KC * 128 * 2, BF16)
    wd_r = K.ring("wd", 3, 4 * 256 * 2, BF16)
    xt_r = K.ring("xt", 2, D * 4)
    xs_r = K.ring("xs", 2, D * 2, BF16)
    small = K.ring("small", 8, 32)
    sg_r = K.ring("sg", 3, 512 * 4)
    xb_r = K.ring("xb", 4, 256 * 4)
    hb_r = K.ring("hb", 2, 256 * 4)
    ctr = [0]
    env = dict(xt_r=xt_r, xs_r=xs_r, small=small, ctr=ctr)
    env["post"] = post_setup(env)
    wgv = Wg.ap.rearrange("(kc p) f -> p kc f", p=128)
    wuv = Wu.ap.rearrange("(kc p) f -> p kc f", p=128)
    wdv = Wd.ap.rearrange("(c p) d -> p c d", p=128)
    aT = v3(actT.ap, NFC)
    xT = v3(xnT.ap, KC)
    wi = 0
    di = 0
    ei = 0
    xi = 0

    def subs_of(t):
        out = []
        toff = 0
        for (r0, n) in t:
            out.append((r0, n, toff))
            toff += n
        return out

    norm_transpose(K, src, subs_of(tiles[0]), gain, gcol, xnT, ident, xt_r, xs_r, small, None, ctr)
    pending = []
    for ti, tile in enumerate(tiles):
        subs = subs_of(tile)
        NT = sum(x_[1] for x_ in subs)
        sts = [(s0, min(512, NT - s0)) for s0 in range(0, NT, 512)]
        for blk in range(NFC):
            wg = wg_r[wi % 3]
            wu = wu_r[wi % 3]
            wi += 1
            S.dma("pool", v3(wg.ap, KC), wgv[:, :, blk * 128:(blk + 1) * 128], reads=[Wg], writes=[wg])
            S.dma("pool", v3(wu.ap, KC), wuv[:, :, blk * 128:(blk + 1) * 128], reads=[Wu], writes=[wu])
            wg3 = v3(wg.ap, KC)
            wu3 = v3(wu.ap, KC)
            for (s0, nn) in sts:
                pg = K.nextps()
                pu = K.nextps()
                for kc in range(KC):
                    S.c("pe", lambda h, pg=pg, wg3=wg3, kc=kc, s0=s0, nn=nn: h.matmul(pg.ap[:, :nn], wg3[:, kc, :], xT[:, kc, s0:s0 + nn], start=(kc == 0), stop=(kc == KC - 1)),
                        reads=[wg, xnT], writes=[pg])
                for kc in range(KC):
                    S.c("pe", lambda h, pu=pu, wu3=wu3, kc=kc, s0=s0, nn=nn: h.matmul(pu.ap[:, :nn], wu3[:, kc, :], xT[:, kc, s0:s0 + nn], start=(kc == 0), stop=(kc == KC - 1)),
                        reads=[wu, xnT], writes=[pu])
                sg = sg_r[ei % 3]
                ei += 1
                S.c("act", lambda h, sg=sg, pg=pg, nn=nn: h.activation(sg.ap[:, :nn], pg.ap[:, :nn], AF.Silu), reads=[pg], writes=[sg])
                S.c("dve", lambda h, blk=blk, s0=s0, nn=nn, sg=sg, pu=pu: h.tensor_tensor(aT[:, blk, s0:s0 + nn], sg.ap[:, :nn], pu.ap[:, :nn], ALU.mult),
                    reads=[sg, pu], writes=[actT])
            if pending and blk % 5 == 4:
                post_sub(pending.pop(0), env)
        while pending:
            post_sub(pending.pop(0), env)
        nxt = subs_of(tiles[ti + 1]) if ti + 1 < len(tiles) else None
        for r in range(8):
            for cg in range(0, NFC, 4):
                ncg = min(4, NFC - cg)
                wd = wd_r[di % 3]
                di += 1
                wd3 = v3(wd.ap, 4)
                S.dma("pool", wd3[:, :ncg, :], wdv[:, cg:cg + ncg, r * 256:(r + 1) * 256], reads=[Wd], writes=[wd])
                for i in range(ncg):
                    c = cg + i
                    for si, (r0, n, toff) in enumerate(subs):
                        bank = K.ps[si // 2]
                        c0 = (si % 2) * 256
                        S.c("pe", lambda h, bank=bank, c0=c0, c=c, toff=toff, n=n, wd3=wd3, i=i, si=si: h.matmul(bank.ap[:n, c0:c0 + 256], aT[:, c, toff:toff + n], wd3[:, i, :], skip_group_check=True, start=(c == 0 and (si % 2 == 0)), stop=(c == NFC - 1 and (si % 2 == 1 or si == len(subs) - 1))),
                            reads=[actT, wd], writes=[bank])
            xbs = {}

            def ld(si, r=r, xbs=xbs):
                r0, n, toff = subs[si]
                xb = xb_r[(xi + si) % 4]
                S.dma("sp", xb.ap[:n, :], src.ap[r0:r0 + n, r * 256:(r + 1) * 256], reads=[src], writes=[xb])
                xbs[si] = xb
            for si in range(min(3, len(subs))):
                ld(si)
            for si, (r0, n, toff) in enumerate(subs):
                if si + 3 < len(subs):
                    ld(si + 3)
                xb = xbs[si]
                hb = hb_r[ei % 2]
                ei += 1
                bank = K.ps[si // 2]
                c0 = (si % 2) * 256
                S.c("dve", lambda h, hb=hb, bank=bank, c0=c0, xb=xb, n=n: h.scalar_tensor_tensor(out=hb.ap[:n, :], in0=bank.ap[:n, c0:c0 + 256], scalar=0.5, in1=xb.ap[:n, :], op0=ALU.mult, op1=ALU.add),
                    reads=[bank, xb], writes=[hb])
                S.dma("act", dst.ap[r0:r0 + n, r * 256:(r + 1) * 256], hb.ap[:n, :], reads=[hb], writes=[dst])
            xi += len(subs)
            if nxt is not None and r < len(nxt):
                norm_transpose(K, src, [nxt[r]], gain, gcol, xnT, ident, xt_r, xs_r, small, None, ctr, psr=(4, 8))
        pending = list(subs)
    while pending:
        post_sub(pending.pop(0), env)
    S.barrier()
    K.off = mark


NFM = 24
NTMB = 14
TMW = 3584
SUBS33 = [(s * 128, 128) for s in range(32)] + [(4096, 16)]
TT9 = [(t * 512, 512) for t in range(8)] + [(4096, 16)]


def load_uT(K, u_all, u_meta, uT):
    S = K.S
    u3 = v3(uT.ap, KC)
    ua = u_all.ap.rearrange("(i r q p) t -> r p i q t", i=4, r=2, q=4, p=128)
    for r in range(2):
        for i in range(4):
            S.dma("sp" if (r * 4 + i) % 2 == 0 else "act", u3[:, i * 4:(i + 1) * 4, r * 2048:(r + 1) * 2048], ua[r, :, i, :, :], reads=[u_all], writes=[uT])
    um = u_meta.ap.rearrange("(c p) t -> p c t", p=128)
    S.dma("sp", u3[:, :, 4096:4112], um, reads=[u_meta], writes=[uT])


def stage_b1(K, u_all, u_meta, win_fm, win_lr, win_tm, wa2, ba, pfm, ptm, Lg):
    S = K.S
    mark = K.off
    uT = K.alloc("uT", KC * LTOT * 2, BF16)
    u3 = v3(uT.ap, KC)
    load_uT(K, u_all, u_meta, uT)
    ei = [0]

    def evac(out_ap, in_ap, scale, reads, writes):
        ei[0] += 1
        if ei[0] % 2 == 0:
            S.c("act", lambda h: h.activation(out_ap, in_ap, AF.Copy, scale=float(scale)), reads=reads, writes=writes)
        else:
            S.c("dve", lambda h: h.tensor_scalar(out_ap, in_ap, float(scale), None, ALU.mult), reads=reads, writes=writes)

    m2 = K.off
    wlr = K.alloc("wlr", KC * 32 * 2, BF16)
    wlr3 = v3(wlr.ap, KC)
    S.dma("pool", wlr3, win_lr.ap.rearrange("(kc p) f -> p kc f", p=128), reads=[win_lr], writes=[wlr])
    wa2b = K.alloc("wa2b", 1024 * 2, BF16)
    bab = K.alloc("bab", 1024 * 2, BF16)
    ones = K.alloc("ones", 128 * 2, BF16)
    onef = K.alloc("onef", 4)
    S.dma("pool", wa2b.ap[:16, :], wa2.ap, reads=[wa2], writes=[wa2b])
    S.dma("pool", bab.ap[:1, :], ba.ap, reads=[ba], writes=[bab])
    S.c("dve", lambda h: h.memset(ones.ap, 1.0), writes=[ones])
    S.c("dve", lambda h: h.memset(onef.ap, 1.0), writes=[onef])
    plT = [K.alloc("plT", LTOT * 2, BF16) for _ in range(2)]
    ez_r = K.ring("ez", 2, 512 * 4)
    Lt_r = K.ring("Lt", 2, 512 * 4)
    for d_ in range(2):
        for (t0, nn) in TT9:
            ps = K.nextps()
            for kc in range(KC):
                S.c("pe", lambda h, ps=ps, kc=kc, d_=d_, t0=t0, nn=nn: h.matmul(ps.ap[:16, :nn], wlr3[:, kc, d_ * 16:(d_ + 1) * 16], u3[:, kc, t0:t0 + nn], start=(kc == 0), stop=(kc == KC - 1)),
                    reads=[wlr, uT], writes=[ps])
            evac(plT[d_].ap[:16, t0:t0 + nn], ps.ap[:16, :nn], 1.0, [ps], [plT[d_]])
    k = 0
    for d_ in range(2):
        for (t0, n) in SUBS33:
            ps = K.nextps()
            S.c("pe", lambda h, ps=ps, d_=d_, t0=t0, n=n: h.matmul(ps.ap[:n, :], plT[d_].ap[:16, t0:t0 + n], wa2b.ap[:16, d_ * 512:(d_ + 1) * 512], start=True, stop=False),
                reads=[plT[d_], wa2b], writes=[ps])
            S.c("pe", lambda h, ps=ps, d_=d_, n=n: h.matmul(ps.ap[:n, :], ones.ap[0:1, :n], bab.ap[0:1, d_ * 512:(d_ + 1) * 512], start=False, stop=True),
                reads=[ones, bab], writes=[ps])
            ez = ez_r[k % 2]
            Lt = Lt_r[k % 2]
            k += 1
            S.c("act", lambda h, ez=ez, ps=ps, n=n: h.activation(ez.ap[:n, :], ps.ap[:n, :], AF.Exp, scale=-1.0), reads=[ps], writes=[ez])
            S.c("act", lambda h, ez=ez, Lt=Lt, n=n: h.activation(Lt.ap[:n, :], ez.ap[:n, :], AF.Ln, bias=onef.ap[:n, 0:1]), reads=[ez, onef], writes=[Lt])
            S.dma("sp", Lg.ap[t0:t0 + n, d_ * 512:(d_ + 1) * 512], Lt.ap[:n, :], reads=[Lt], writes=[Lg])
    S.barrier()
    K.off = m2

    wb_r = K.ring("wb", 3, KC * 128 * 2, BF16)
    ofm_r = K.ring("ofm", 2, LTOT * 2, BF16)
    wfv = win_fm.ap.rearrange("(kc p) f -> p kc f", p=128)
    for ci in range(NFM):
        if ci < 8:
            sc = (1.0 / 16.0) if (ci % 4) < 2 else 1.0
        else:
            sc = (128.0 ** -0.5) if (ci % 2) == 0 else 1.0
        wb = wb_r[ci % 3]
        wb3 = v3(wb.ap, KC)
        S.dma("pool", wb3, wfv[:, :, ci * 128:(ci + 1) * 128], reads=[win_fm], writes=[wb])
        ofm = ofm_r[ci % 2]
        for (t0, nn) in TT9:
            ps = K.nextps()
            for kc in range(KC):
                S.c("pe", lambda h, ps=ps, kc=kc, wb3=wb3, t0=t0, nn=nn: h.matmul(ps.ap[:, :nn], wb3[:, kc, :], u3[:, kc, t0:t0 + nn], start=(kc == 0), stop=(kc == KC - 1)),
                    reads=[wb, uT], writes=[ps])
            evac(ofm.ap[:, t0:t0 + nn], ps.ap[:, :nn], sc, [ps], [ofm])
        S.dma("sp", pfm.ap[ci], ofm.ap, reads=[ofm], writes=[pfm])
    S.barrier()
    K.off = m2

    wt_r = K.ring("wt", 2, KC * 256 * 2, BF16)
    otm_r = K.ring("otm", 2, 33 * 256 * 2, BF16)
    wtv = win_tm.ap.rearrange("(kc p) f -> p kc f", p=128)
    ptr = ptm.ap[0:4096, :].rearrange("(s p) c -> p s c", p=128)
    for blk in range(NTMB):
        wt = wt_r[blk % 2]
        wt3 = v3(wt.ap, KC)
        S.dma("pool", wt3, wtv[:, :, blk * 256:(blk + 1) * 256], reads=[win_tm], writes=[wt])
        otm = otm_r[blk % 2]
        o3 = v3(otm.ap, 33)
        for si, (t0, n) in enumerate(SUBS33):
            ps = K.nextps()
            for kc in range(KC):
                S.c("pe", lambda h, ps=ps, kc=kc, wt3=wt3, t0=t0, n=n: h.matmul(ps.ap[:n, :256], u3[:, kc, t0:t0 + n], wt3[:, kc, :], start=(kc == 0), stop=(kc == KC - 1)),
                    reads=[wt, uT], writes=[ps])
            evac(o3[:n, si, :], ps.ap[:n, :256], 1.0, [ps], [otm])
        S.dma("sp", ptr[:, :, blk * 256:(blk + 1) * 256], o3[:, 0:32, :], reads=[otm], writes=[ptm])
        S.dma("sp", ptm.ap[4096:4112, blk * 256:(blk + 1) * 256], o3[:16, 32, :], reads=[otm], writes=[ptm])
    S.barrier()
    K.off = mark


def rms_rstd(K, small, col, ss, n, width):
    S = K.S
    rs = small.ap[:n, 16 + col:17 + col]
    S.c("dve", lambda h: h.tensor_scalar(rs, ss, 1.0 / width, EPS, ALU.mult, ALU.add), reads=[small], writes=[small])
    S.c("act", lambda h: h.activation(rs, rs, AF.Sqrt), reads=[small], writes=[small])
    S.c("dve", lambda h: h.reciprocal(rs, rs), reads=[small], writes=[small])
    return rs


def stage_b3(K, pfm, ptm, ttab, slopes_in, lamv, dnw_in, y_send, ident):
    S = K.S
    mark = K.off
    Tt = K.alloc("Tt", TW * 4)
    S.dma("sp", Tt.ap, ttab.ap, reads=[ttab], writes=[Tt])
    slp = K.alloc("slp", 16)
    S.dma("sp", slp.ap, slopes_in.ap, reads=[slopes_in], writes=[slp])
    lv = K.alloc("lv", 512 * 4)
    S.dma("sp", lv.ap, lamv.ap, reads=[lamv], writes=[lv])
    nw = K.alloc("nw", 256 * 4)
    S.dma("sp", nw.ap, dnw_in.ap, reads=[dnw_in], writes=[nw])
    S.c("dve", lambda h: h.tensor_scalar(nw.ap, nw.ap, 0.8, None, ALU.mult), reads=[nw], writes=[nw])
    small = K.alloc("small3", 64 * 4)
    pr = K.alloc("pr", 128 * 4)
    for i in range(2):
        S.c("dve", lambda h, i=i: h.tensor_tensor(pr.ap, lv.ap[:, i * 256:i * 256 + 128], lv.ap[:, i * 256 + 128:i * 256 + 256], ALU.mult), reads=[lv], writes=[pr])
        S.c("dve", lambda h, i=i: h.reduce_sum(small.ap[:, 1 + i:2 + i], pr.ap, AX.X), reads=[pr], writes=[small])
    S.c("act", lambda h: h.activation(small.ap[:, 1:3], small.ap[:, 1:3], AF.Exp), reads=[small], writes=[small])
    S.c("dve", lambda h: h.tensor_tensor(small.ap[:, 0:1], small.ap[:, 2:3], small.ap[:, 1:2], ALU.subtract), reads=[small], writes=[small])
    S.c("dve", lambda h: h.tensor_scalar(small.ap[:, 0:1], small.ap[:, 0:1], -0.2, None, ALU.add), reads=[small], writes=[small])
    neglam = small.ap[:, 0:1]

    kT = [K.alloc("kT", LTOT * 2, BF16) for _ in range(2)]
    qT = [K.alloc("qT", 4096 * 2, BF16) for _ in range(2)]
    vaug = K.alloc("vaug", 33 * 257 * 2, BF16)
    va3 = vaug.ap[:, 0:33 * 257].rearrange("p (a b) -> p a b", a=33)
    sb_r = K.ring("sb", 4, 512 * 4)
    pT_r = K.ring("pT", 4, 512 * 2, BF16)
    om = [K.alloc("om", 4 * 257 * 4) for _ in range(2)]
    a_t = K.alloc("a_t", 256 * 4)
    o_t = K.alloc("o_t", 256 * 4)
    jk = K.alloc("jk", 256 * 4)
    yb_r = K.ring("yb", 4, 256 * 2, BF16)
    yT = K.alloc("yTd", 2 * 4096 * 2, BF16)
    yT3 = v3(yT.ap, 2)
    ysv = y_send.ap.rearrange("i (c p) t -> i p c t", p=128)
    ptr = ptm.ap[0:4096, :].rearrange("(s p) c -> p s c", p=128)
    it = 0
    LA = 2
    for d in range(4):
        for m in range(2):
            S.dma("sp", kT[m].ap, pfm.ap[8 + d * 4 + m * 2 + 1], reads=[pfm], writes=[kT[m]])
            S.dma("act", qT[m].ap, pfm.ap[8 + d * 4 + m * 2][:, 0:4096], reads=[pfm], writes=[qT[m]])
        S.dma("sp", va3[:, 0:32, 0:256], ptr[:, :, 2560 + d * 256:2560 + (d + 1) * 256], reads=[ptm], writes=[vaug])
        S.dma("sp", va3[:16, 32, 0:256], ptm.ap[4096:4112, 2560 + d * 256:2560 + (d + 1) * 256], reads=[ptm], writes=[vaug])
        S.c("dve", lambda h: h.memset(va3[:, :, 256:257], 1.0), writes=[vaug])
        accs = [K.ps[a_] for a_ in range(4)]
        items = [(qt, m, kt) for qt in range(8) for m in range(2) for kt in range(33)]
        pts = {}
        deferred = []

        def emit_qk(idx):
            qt, m, kt = items[idx]
            k0, nk = SUBS33[kt]
            ps = K.nextps(4, 8)
            S.c("pe", lambda h, ps=ps, m=m, k0=k0, nk=nk, qt=qt: h.matmul(ps.ap[:nk, :], kT[m].ap[:, k0:k0 + nk], qT[m].ap[:, qt * 512:(qt + 1) * 512], start=True, stop=True),
                reads=[kT[m], qT[m]], writes=[ps])
            pk0 = (16 + k0) if kt < 32 else 0
            off = (16 + 512 * qt) - pk0 + C0
            sb = sb_r[idx % len(sb_r)]
            pT = pT_r[idx % len(pT_r)]
            S.c("dve", lambda h, sb=sb, ps=ps, nk=nk, off=off, d=d: h.scalar_tensor_tensor(out=sb.ap[:nk, :], in0=Tt.ap[:nk, off:off + 512], scalar=slp.ap[:nk, d:d + 1], in1=ps.ap[:nk, :], op0=ALU.mult, op1=ALU.add),
                reads=[Tt, slp, ps], writes=[sb])
            S.c("act", lambda h, sb=sb, pT=pT, nk=nk: h.activation(pT.ap[:nk, :], sb.ap[:nk, :], AF.Exp), reads=[sb], writes=[pT])
            pts[idx] = pT

        def emit_av(idx):
            qt, m, kt = items[idx]
            k0, nk = SUBS33[kt]
            pT = pts.pop(idx)
            for qs in range(4):
                S.c("pe", lambda h, acc=accs[qs], pT=pT, nk=nk, qs=qs, kt=kt: h.matmul(acc.ap[:, 0:257], pT.ap[:nk, qs * 128:(qs + 1) * 128], va3[:nk, kt, :], start=(kt == 0), stop=(kt == 32)),
                    reads=[pT, vaug], writes=[accs[qs]])
            if kt == 32:
                o3m = v3(om[m].ap, 4)
                for qs in range(4):
                    if qs % 2 == 0:
                        S.c("act", lambda h, o3m=o3m, qs=qs, acc=accs[qs]: h.activation(o3m[:, qs, :], acc.ap[:, 0:257], AF.Copy), reads=[accs[qs]], writes=[om[m]])
                    else:
                        S.c("dve", lambda h, o3m=o3m, qs=qs, acc=accs[qs]: h.tensor_copy(o3m[:, qs, :], acc.ap[:, 0:257]), reads=[accs[qs]], writes=[om[m]])
                if m == 1:
                    combine(qt, idx)

        def combine(qt, idx):
            o30 = v3(om[0].ap, 4)
            o31 = v3(om[1].ap, 4)
            for qs in range(4):
                S.c("dve", lambda h, qs=qs: h.reciprocal(small.ap[:, 4:5], o30[:, qs, 256:257]), reads=[om[0]], writes=[small])
                S.c("dve", lambda h, qs=qs: h.reciprocal(small.ap[:, 5:6], o31[:, qs, 256:257]), reads=[om[1]], writes=[small])
                S.c("dve", lambda h: h.tensor_tensor(small.ap[:, 5:6], small.ap[:, 5:6], neglam, ALU.mult), reads=[small], writes=[small])
                S.c("dve", lambda h, qs=qs: h.tensor_scalar(a_t.ap, o30[:, qs, 0:256], small.ap[:, 4:5], None, ALU.mult), reads=[om[0], small], writes=[a_t])
                S.c("dve", lambda h, qs=qs: h.scalar_tensor_tensor(out=o_t.ap, in0=o31[:, qs, 0:256], scalar=small.ap[:, 5:6], in1=a_t.ap, op0=ALU.mult, op1=ALU.add),
                    reads=[om[1], small, a_t], writes=[o_t])
                S.c("dve", lambda h: h.memset(small.ap[:, 6:7], 0.0), writes=[small])
                S.c("act", lambda h: h.activation(jk.ap, o_t.ap, AF.Square, accum_out=small.ap[:, 6:7]), reads=[o_t, small], writes=[jk, small])
                rs = rms_rstd(K, small, 0, small.ap[:, 6:7], 128, 256.0)
                yb = yb_r[qs]
                S.c("dve", lambda h, yb=yb, rs=rs: h.scalar_tensor_tensor(out=yb.ap, in0=o_t.ap, scalar=rs, in1=nw.ap, op0=ALU.mult, op1=ALU.mult),
                    reads=[o_t, small, nw], writes=[yb])

                def tr(yb=yb, qs=qs, qt=qt):
                    ps = K.nextps(4, 8)
                    psb = ps.ap.bitcast(BF16)
                    for fc in range(2):
                        S.c("pe", lambda h, psb=psb, fc=fc, yb=yb: h.transpose(psb[:, fc * 128:(fc + 1) * 128], yb.ap[:, fc * 128:(fc + 1) * 128], ident.ap), reads=[yb, ident], writes=[ps])
                    q0 = qt * 512 + qs * 128
                    S.c("act", lambda h, psb=psb, q0=q0: h.activation(yT3[:, :, q0:q0 + 128], v3(psb[:, 0:256], 2), AF.Copy), reads=[ps], writes=[yT])
                deferred.append((idx + 8 + 2 * qs, tr))

        n = len(items)
        for idx in range(n + LA):
            if idx < n:
                emit_qk(idx)
            if idx - LA >= 0:
                emit_av(idx - LA)
            while deferred and deferred[0][0] <= idx:
                deferred.pop(0)[1]()
        while deferred:
            deferred.pop(0)[1]()
        for i8 in range(8):
            S.dma("sp", ysv[i8][:, 8 + d * 2:10 + d * 2, :], yT3[:, :, i8 * 512:(i8 + 1) * 512], reads=[yT], writes=[y_send])
    S.barrier()
    K.off = mark


def stage_b2(K, pfm, ptm, Lg, gmask_in, gnw_in, y_send, ident):
    S = K.S
    mark = K.off
    gm = K.alloc("gm", 6 * 128 * 4)
    S.dma("sp", gm.ap, gmask_in.ap, reads=[gmask_in], writes=[gm])
    TRI = [gm.ap[:, 0:128], gm.ap[:, 256:384]]
    UPP = [gm.ap[:, 128:256], gm.ap[:, 384:512]]
    MSK = [gm.ap[:, 512:640], gm.ap[:, 640:768]]
    gnw = K.alloc("gnw", 512 * 4)
    S.dma("sp", gnw.ap, gnw_in.ap, reads=[gnw_in], writes=[gnw])
    small = K.alloc("small2", 64 * 4)
    qT = K.alloc("gqT", 2 * 4096 * 2, BF16); q3 = v3(qT.ap, 2)
    kT = K.alloc("gkT", 2 * LTOT * 2, BF16); k3 = v3(kT.ap, 2)
    ktm = K.alloc("gktm", 33 * 256 * 2, BF16); kt3 = v3(ktm.ap, 33)
    vtm = K.alloc("gvtm", 33 * 512 * 2, BF16); vt3 = v3(vtm.ap, 33)
    ofw = K.alloc("ofw", 32 * 512 * 2, BF16); of3 = v3(ofw.ap, 32)
    st_d = [[K.alloc("st", 512 * 4) for _ in range(2)] for _ in range(2)]
    stb_d = [[[K.alloc("stb", 512 * 2, BF16) for _ in range(2)] for _ in range(2)] for _ in range(2)]
    qdA_r = K.ring("qdA", 4, 256 * 2, BF16)
    qdB_r = K.ring("qdB", 4, 256 * 2, BF16)
    kiT_r = K.ring("kiT", 4, 256 * 2, BF16)
    kend_r = K.ring("kend", 4, 256 * 2, BF16)
    sm_r = K.ring("sm", 4, 128 * 2, BF16)
    Lt_r = K.ring("gLt", 4, 256 * 4)
    eb_r = K.ring("eb", 4, 256 * 4)
    enb_r = K.ring("enb", 4, 256 * 4)
    eE_r = K.ring("eE", 4, 256 * 4)
    rt_r = K.ring("rt", 2, 512 * 2, BF16)
    sr_r = K.ring("sr", 2, 512 * 4)
    of_r = K.ring("of", 2, 512 * 4)
    jk = K.alloc("gjk", 512 * 4)
    yb_r = K.ring("gyb", 2, 512 * 2, BF16)
    yt_r = K.ring("gyt", 2, 512 * 2, BF16)
    for t in qdA_r + qdB_r:
        S.c("dve", lambda h, t=t: h.memset(t.ap, 0.0), writes=[t])
    ysv = y_send.ap.rearrange("i (c p) t -> i p c t", p=128)
    ptr = ptm.ap[0:4096, :].rearrange("(s p) c -> p s c", p=128)
    cnt = [0]
    stcur = [0, 0]

    def state_update(g, kend, i, ch, nrows, eb3, col, dr):
        r0 = ch * 64
        stcur[dr] ^= 1
        pks = []
        for dc in range(2):
            pk = K.nextps(2, 8)
            S.c("pe", lambda h, pk=pk, dc=dc: h.matmul(pk.ap[:, :], kend.ap[r0:r0 + nrows, dc * 128:(dc + 1) * 128], vt3[r0:r0 + nrows, i, :], start=True, stop=True),
                reads=[kend, vtm], writes=[pk])
            pks.append(pk)
        for dc in range(2):
            pk = pks[dc]
            st = st_d[dr][dc]
            if eb3 is None:
                S.c("dve", lambda h, pk=pk, st=st: h.tensor_copy(st.ap, pk.ap[:, :]), reads=[pk], writes=[st])
            else:
                S.c("dve", lambda h, pk=pk, dc=dc, st=st: h.scalar_tensor_tensor(out=st.ap, in0=st.ap, scalar=eb3[0][:, dc, col:col + 1], in1=pk.ap[:, :], op0=ALU.mult, op1=ALU.add),
                    reads=[st, eb3[1], pk], writes=[st])
            nb = stb_d[dr][stcur[dr]][dc]
            S.c("act", lambda h, nb=nb, st=st: h.activation(nb.ap, st.ap, AF.Copy), reads=[st], writes=[nb])

    for g in range(2):
        for dc in range(2):
            S.dma("sp", q3[:, dc, :], pfm.ap[g * 4 + dc][:, 0:4096], reads=[pfm], writes=[qT])
            S.dma("act", k3[:, dc, :], pfm.ap[g * 4 + 2 + dc], reads=[pfm], writes=[kT])
        S.dma("sp", kt3[:, 0:32, :], ptr[:, :, g * 256:(g + 1) * 256], reads=[ptm], writes=[ktm])
        S.dma("sp", kt3[:16, 32, :], ptm.ap[4096:4112, g * 256:(g + 1) * 256], reads=[ptm], writes=[ktm])
        S.dma("act", vt3[:, 0:32, :], ptr[:, :, 512 + g * 512:512 + (g + 1) * 512], reads=[ptm], writes=[vtm])
        S.dma("act", vt3[:16, 32, :], ptm.ap[4096:4112, 512 + g * 512:512 + (g + 1) * 512], reads=[ptm], writes=[vtm])
        if True:
            if True:
                Lt = Lt_r[cnt[0] % 4]
                eE = eE_r[cnt[0] % 4]
                kend = kend_r[cnt[0] % 4]
                cnt[0] += 1
                S.dma("sp", Lt.ap[:16, :], Lg.ap[4096:4112, g * 256:(g + 1) * 256], reads=[Lg], writes=[Lt])
                pa = K.nextps(2, 8)
                S.c("pe", lambda h, pa=pa, Lt=Lt: h.matmul(pa.ap[:16, 0:256], UPP[0][:16, :16], Lt.ap[:16, :], start=True, stop=True), reads=[gm, Lt], writes=[pa])
                S.c("act", lambda h, pa=pa, eE=eE: h.activation(eE.ap[:16, :], pa.ap[:16, 0:256], AF.Exp), reads=[pa], writes=[eE])
                S.c("dve", lambda h, kend=kend, eE=eE: h.tensor_tensor(kend.ap[:16, :], kt3[:16, 32, :], eE.ap[:16, :], ALU.mult), reads=[ktm, eE], writes=[kend])
                state_update(g, kend, 32, 0, 16, None, 0, 0)
                stcur[1] ^= 1
                for dc_ in range(2):
                    S.c("dve", lambda h, dc_=dc_: h.memset(st_d[1][dc_].ap, 0.0), writes=[st_d[1][dc_]])
                    nb = stb_d[1][stcur[1]][dc_]
                    S.c("dve", lambda h, nb=nb: h.memset(nb.ap, 0.0), writes=[nb])
            def pre(i, dr, g=g):
                    t0 = i * 128
                    c_ = cnt[0]
                    cnt[0] += 1
                    Lt = Lt_r[c_ % 4]; eb = eb_r[c_ % 4]; enb = enb_r[c_ % 4]; eE = eE_r[c_ % 4]
                    qdA = qdA_r[c_ % 4]; qdB = qdB_r[c_ % 4]; kiT = kiT_r[c_ % 4]; kend = kend_r[c_ % 4]; sm = sm_r[c_ % 4]
                    S.dma("sp", Lt.ap, Lg.ap[t0:t0 + 128, dr * 512 + g * 256:dr * 512 + (g + 1) * 256], reads=[Lg], writes=[Lt])
                    pa = K.nextps(2, 8)
                    for dc in range(2):
                        S.c("pe", lambda h, pa=pa, dc=dc, Lt=Lt, dr=dr: h.matmul(pa.ap[:, dc * 128:(dc + 1) * 128], Lt.ap[:, dc * 128:(dc + 1) * 128], TRI[dr], start=True, stop=True),
                            reads=[Lt, gm], writes=[pa])
                    S.c("pe", lambda h, pa=pa, Lt=Lt, dr=dr: h.matmul(pa.ap[:, 256:512], UPP[dr], Lt.ap[:, :], start=True, stop=True), reads=[Lt, gm], writes=[pa])
                    S.c("act", lambda h, pa=pa, eb=eb: h.activation(eb.ap, pa.ap[:, 0:256], AF.Exp), reads=[pa], writes=[eb])
                    S.c("act", lambda h, pa=pa, enb=enb: h.activation(enb.ap, pa.ap[:, 0:256], AF.Exp, scale=-1.0), reads=[pa], writes=[enb])
                    S.c("act", lambda h, pa=pa, eE=eE: h.activation(eE.ap, pa.ap[:, 256:512], AF.Exp), reads=[pa], writes=[eE])
                    eb3 = v3(eb.ap, 2); enb3 = v3(enb.ap, 2)
                    qa3 = v3(qdA.ap, 2); qb3 = v3(qdB.ap, 2); ki3 = v3(kiT.ap, 2)
                    S.c("dve", lambda h, qa3=qa3, eb3=eb3, t0=t0: h.tensor_tensor(qa3[:, :, 0:64], q3[:, :, t0:t0 + 64], eb3[:, :, 0:64], ALU.mult), reads=[qT, eb], writes=[qdA])
                    S.c("dve", lambda h, qb3=qb3, eb3=eb3, t0=t0: h.tensor_tensor(qb3[:, :, 64:128], q3[:, :, t0 + 64:t0 + 128], eb3[:, :, 64:128], ALU.mult), reads=[qT, eb], writes=[qdB])
                    S.c("dve", lambda h, ki3=ki3, enb3=enb3, t0=t0: h.tensor_tensor(ki3, k3[:, :, t0:t0 + 128], enb3, ALU.mult), reads=[kT, enb], writes=[kiT])
                    S.c("dve", lambda h, kend=kend, eE=eE, i=i: h.tensor_tensor(kend.ap, kt3[:, i, :], eE.ap, ALU.mult), reads=[ktm, eE], writes=[kend])
                    psS = K.nextps(2, 8)
                    n_ = 0
                    for dc in range(2):
                        for (qd, q3_) in ((qdA, qa3), (qdB, qb3)):
                            S.c("pe", lambda h, psS=psS, ki3=ki3, dc=dc, q3_=q3_, n_=n_: h.matmul(psS.ap[:, 0:128], ki3[:, dc, :], q3_[:, dc, :], start=(n_ == 0), stop=(n_ == 3)),
                                reads=[kiT, qd], writes=[psS])
                            n_ += 1
                    S.c("dve", lambda h, sm=sm, psS=psS, dr=dr: h.tensor_tensor(sm.ap, psS.ap[:, 0:128], MSK[dr], ALU.mult), reads=[psS, gm], writes=[sm])

                    return dict(i=i, t0=t0, c_=c_, eb=eb, eb3=eb3, qdA=qdA, qdB=qdB, qa3=qa3, qb3=qb3, kend=kend, sm=sm)

            def chain(cx, dr, first, part, g=g):
                    i = cx["i"]; t0 = cx["t0"]; c_ = cx["c_"]; eb = cx["eb"]; eb3 = cx["eb3"]; qdA = cx["qdA"]; qdB = cx["qdB"]
                    qa3 = cx["qa3"]; qb3 = cx["qb3"]; kend = cx["kend"]; sm = cx["sm"]
                    chs = [0, 1] if dr == 0 else [1, 0]
                    if part == 0:
                        psO = K.ps[dr]
                        cx["psO"] = psO
                        S.c("pe", lambda h, psO=psO, sm=sm, i=i: h.matmul(psO.ap[:, :], sm.ap, vt3[:, i, :], start=True, stop=False), reads=[sm, vtm], writes=[psO])
                    psO = cx["psO"]
                    for ci_, ch in enumerate(chs):
                        if ci_ != part:
                            continue
                        qd, q3_ = (qdA, qa3) if ch == 0 else (qdB, qb3)
                        for dc in range(2):
                            sbt = stb_d[dr][stcur[dr]][dc]
                            S.c("pe", lambda h, psO=psO, q3_=q3_, dc=dc, sbt=sbt, last=(ci_ == 1 and dc == 1): h.matmul(psO.ap[:, :], q3_[:, dc, :], sbt.ap, start=False, stop=last),
                                reads=[qd, sbt], writes=[psO])
                        col = ch * 64 + (63 if dr == 0 else 0)
                        state_update(g, kend, i, ch, 64, (eb3, eb), col, dr)
                    if part == 0:
                        return
                    if first:
                        S.c("act", lambda h, psO=psO, i=i: h.activation(of3[:, i, :], psO.ap[:, :], AF.Copy), reads=[psO], writes=[ofw])
                    else:
                        of = of_r[c_ % 2]; rt = rt_r[c_ % 2]; sr = sr_r[c_ % 2]; yb = yb_r[c_ % 2]; yt = yt_r[c_ % 2]
                        S.dma("act", rt.ap, ptm.ap[t0:t0 + 128, 1536 + g * 512:1536 + (g + 1) * 512], reads=[ptm], writes=[rt])
                        S.c("dve", lambda h, of=of, psO=psO, i=i: h.tensor_tensor(of.ap, psO.ap[:, :], of3[:, i, :], ALU.add), reads=[psO, ofw], writes=[of])
                        S.c("dve", lambda h: h.memset(small.ap[:, 6:7], 0.0), writes=[small])
                        S.c("act", lambda h, of=of: h.activation(jk.ap, of.ap, AF.Square, accum_out=small.ap[:, 6:7]), reads=[of, small], writes=[jk, small])
                        rs = rms_rstd(K, small, 0, small.ap[:, 6:7], 128, 512.0)
                        S.c("act", lambda h, sr=sr, rt=rt: h.activation(sr.ap, rt.ap, AF.Silu), reads=[rt], writes=[sr])
                        S.c("dve", lambda h, of=of, rs=rs: h.scalar_tensor_tensor(out=of.ap, in0=of.ap, scalar=rs, in1=gnw.ap, op0=ALU.mult, op1=ALU.mult), reads=[of, small, gnw], writes=[of])
                        S.c("dve", lambda h, yb=yb, of=of, sr=sr: h.tensor_tensor(yb.ap, of.ap, sr.ap, ALU.mult), reads=[of, sr], writes=[yb])
                        pt = K.nextps(2, 8)
                        ptb = pt.ap.bitcast(BF16)
                        for fc in range(4):
                            S.c("pe", lambda h, ptb=ptb, fc=fc, yb=yb: h.transpose(ptb[:, fc * 128:(fc + 1) * 128], yb.ap[:, fc * 128:(fc + 1) * 128], ident.ap), reads=[yb, ident], writes=[pt])
                        S.c("act", lambda h, yt=yt, ptb=ptb: h.activation(yt.ap, ptb[:, 0:512], AF.Copy), reads=[pt], writes=[yt])
                        S.dma("sp", ysv[t0 // 512][:, g * 4:(g + 1) * 4, (t0 % 512):(t0 % 512) + 128], v3(yt.ap, 4), reads=[yt], writes=[y_send])

            seq = []
            for n_ in range(32):
                seq.append((0, n_, n_ < 16))
                seq.append((1, 31 - n_, n_ < 16))
            LAG = 2
            cxs = {}
            for k_ in range(LAG):
                cxs[k_] = pre(seq[k_][1], seq[k_][0])
            for k_ in range(0, len(seq), 2):
                for kk in (k_, k_ + 1):
                    if kk + LAG < len(seq):
                        cxs[kk + LAG] = pre(seq[kk + LAG][1], seq[kk + LAG][0])
                for part in range(2):
                    for kk in (k_, k_ + 1):
                        chain(cxs[kk], seq[kk][0], seq[kk][2], part)
                cxs.pop(k_); cxs.pop(k_ + 1)
    S.barrier()
    K.off = mark


def get_par(h):
    return _PAR["p"]


def stage_c(K, u_send, y_all, h1, wgate, wA, wB, wO, h2, fake_y):
    S = K.S
    mark = K.off
    TT = 1024
    uT = K.alloc("cuT", KC * TT * 2, BF16); u3 = v3(uT.ap, KC)
    mT = K.alloc("mT", KC * TT * 2, BF16); m3 = v3(mT.ap, KC)
    wr = [K.ring("cw%d" % i, 2, KC * 128 * 2, BF16) for i in range(4)]
    tmp_r = [K.ring("ctmp%d" % i, 2, 512 * 4) for i in range(4)]
    m_y = K.off
    yT = K.alloc("cyT", 32 * TT * 2, BF16); y3 = v3(yT.ap, 32)
    usv = u_send.ap.rearrange("(c p) t -> p c t", p=128)
    wgv = wgate.ap.rearrange("(kc p) f -> p kc f", p=128)
    wAv = wA.ap.rearrange("(kc p) f -> p kc f", p=128)
    wBv = wB.ap.rearrange("(kc p) f -> p kc f", p=128)
    wOv = wO.ap.rearrange("(kc p) f -> p kc f", p=128)
    wi = 0
    for tl in range(2):
        t0 = tl * TT
        S.dma("sp", u3, usv[:, :, t0:t0 + TT], reads=[u_send], writes=[uT])
        for r in range(2):
            for ii in range(2):
                S.dmaf("sp", lambda h, r=r, ii=ii, tl=tl: h.dma_start(out=y3[:, r * 16:(r + 1) * 16, ii * 512:(ii + 1) * 512],
                                                                    in_=y_all.ap[bass.ds(_PAR[tl * 2 + ii], 1), r * D:(r + 1) * D, :].rearrange("o (c p) t -> p (o c) t", p=128)),
                       reads=[y_all], writes=[yT])
        for oc in range(KC):
            ws = [wr[i][wi % 2] for i in range(4)]
            wi += 1
            srcs = [wgv[:, :, oc * 128:(oc + 1) * 128], wgv[:, :, D + oc * 128:D + (oc + 1) * 128], wAv[:, :, oc * 128:(oc + 1) * 128], wBv[:, :, oc * 128:(oc + 1) * 128]]
            srcT = [wgate, wgate, wA, wB]
            for i in range(4):
                S.dma("pool", v3(ws[i].ap, KC), srcs[i], reads=[srcT[i]], writes=[ws[i]])
            w3 = [v3(w.ap, KC) for w in ws]
            for st_ in range(2):
                s0 = st_ * 512
                pss = [K.nextps() for _ in range(4)]
                for kc in range(KC):
                    S.c("pe", lambda h, kc=kc, s0=s0, p=pss[0], w=w3[0]: h.matmul(p.ap[:, :], w[:, kc, :], u3[:, kc, s0:s0 + 512], start=(kc == 0), stop=(kc == KC - 1)), reads=[ws[0], uT], writes=[pss[0]])
                for kc in range(KC):
                    S.c("pe", lambda h, kc=kc, s0=s0, p=pss[1], w=w3[1]: h.matmul(p.ap[:, :], w[:, kc, :], u3[:, kc, s0:s0 + 512], start=(kc == 0), stop=(kc == KC - 1)), reads=[ws[1], uT], writes=[pss[1]])
                for kc in range(KC):
                    yi = (kc // 8) * 16 + (kc % 8)
                    S.c("pe", lambda h, kc=kc, yi=yi, s0=s0, p=pss[2], w=w3[2]: h.matmul(p.ap[:, :], w[:, kc, :], y3[:, yi, s0:s0 + 512], start=(kc == 0), stop=(kc == KC - 1)), reads=[ws[2], yT], writes=[pss[2]])
                for kc in range(KC):
                    yi = (kc // 8) * 16 + 8 + (kc % 8)
                    S.c("pe", lambda h, kc=kc, yi=yi, s0=s0, p=pss[3], w=w3[3]: h.matmul(p.ap[:, :], w[:, kc, :], y3[:, yi, s0:s0 + 512], start=(kc == 0), stop=(kc == KC - 1)), reads=[ws[3], yT], writes=[pss[3]])
                tm = [tmp_r[i][(wi + st_) % 2] for i in range(4)]
                S.c("act", lambda h, t=tm[0], p=pss[0]: h.activation(t.ap, p.ap[:, :], AF.Sigmoid), reads=[pss[0]], writes=[tm[0]])
                S.c("act", lambda h, t=tm[1], p=pss[1]: h.activation(t.ap, p.ap[:, :], AF.Sigmoid), reads=[pss[1]], writes=[tm[1]])
                S.c("dve", lambda h, t=tm[2], a=tm[0], p=pss[2]: h.tensor_tensor(t.ap, a.ap, p.ap[:, :], ALU.mult), reads=[tm[0], pss[2]], writes=[tm[2]])
                S.c("dve", lambda h, t=tm[3], a=tm[1], p=pss[3]: h.tensor_tensor(t.ap, a.ap, p.ap[:, :], ALU.mult), reads=[tm[1], pss[3]], writes=[tm[3]])
                S.c("pool", lambda h, oc=oc, s0=s0, a=tm[2], b=tm[3]: h.tensor_tensor(m3[:, oc, s0:s0 + 512], a.ap, b.ap, ALU.add), reads=[tm[2], tm[3]], writes=[mT])
        S.barrier()
        keep = K.off
        K.off = m_y
        wo_r = K.ring("wo", 2, KC * 512 * 2, BF16)
        hb_r = K.ring("chb", 2, 512 * 4)
        ho_r = K.ring("cho", 2, 512 * 4)
        k = 0
        for db in range(4):
            wo = wo_r[db % 2]
            wo3 = v3(wo.ap, KC)
            S.dma("pool", wo3, wOv[:, :, db * 512:(db + 1) * 512], reads=[wO], writes=[wo])
            for s in range(8):
                r0 = t0 + s * 128
                ps = K.nextps()
                for kc in range(KC):
                    S.c("pe", lambda h, ps=ps, kc=kc, s=s, wo3=wo3: h.matmul(ps.ap[:, :], m3[:, kc, s * 128:(s + 1) * 128], wo3[:, kc, :], start=(kc == 0), stop=(kc == KC - 1)), reads=[mT, wo], writes=[ps])
                hb = hb_r[k % 2]; ho = ho_r[k % 2]
                k += 1
                S.dma("sp", hb.ap, h1.ap[r0:r0 + 128, db * 512:(db + 1) * 512], reads=[h1], writes=[hb])
                S.c("dve", lambda h, ho=ho, ps=ps, hb=hb: h.tensor_tensor(ho.ap, ps.ap[:, :], hb.ap, ALU.add), reads=[ps, hb], writes=[ho])
                S.dma("sp", h2.ap[r0:r0 + 128, db * 512:(db + 1) * 512], ho.ap, reads=[ho], writes=[h2])
        S.barrier()
        K.off = keep
    S.barrier()
    K.off = mark


ALL_STAGES = ("A", "B1", "B2", "B3", "C")


def build(stages=ALL_STAGES, dbg=()):
    _PAR.clear()
    nc = bass.Bass("TRN2", target_bir_lowering=False)
    S = Sched(nc)
    K = KB(nc, S)
    st = set(stages)

    def din(name, shape, dt=F32):
        return T(name, nc.dram_tensor(name, list(shape), dt, kind="ExternalInput").ap(), dram=True)

    def dscr(name, shape, dt, fake=False):
        if fake:
            return din(name, shape, dt)
        return T(name, nc.dram_tensor(name, list(shape), dt).ap(), dram=True)

    fA = "A" not in st
    fB1 = "B1" not in st
    fB = not ("B2" in st and "B3" in st)
    gvec = din("gvec", [128, 64])
    out = T("out", nc.dram_tensor("out", [NTOK, D], F32, kind="ExternalOutput").ap(), dram=True)
    h1 = dscr("h1", [NTOK, D], F32, fA)
    u_send = dscr("u_send", [D, NTOK], BF16, fA)
    u_all = dscr("u_all", [2 * D, NTOK], BF16, fA)
    u_meta = dscr("u_meta", [D, NMETA], BF16, fA)
    pfm = dscr("pfm", [NFM, 128, LTOT], BF16, fB1)
    ptm = dscr("ptm", [LTOT, TMW], BF16, fB1)
    Lg = dscr("Lg", [LTOT, 1024], F32, fB1)
    y_send = dscr("y_send", [8, D, 512], BF16)
    y_all = dscr("y_all", [8, 2 * D, 512], BF16, fB)
    h2 = dscr("h2", [NTOK, D], F32)
    h3 = dscr("h3", [NTOK, D], F32)

    ident = K.alloc("ident", 128 * 2, BF16)
    id32 = K.alloc("id32", 128 * 4)
    gain = K.alloc("gain", 64 * 4)
    S.c("pool", lambda h: h.memset(id32.ap, 0.0), writes=[id32])
    S.c("pool", lambda h: h.affine_select(out=id32.ap, in_=id32.ap, pattern=[[-1, 128]], compare_op=ALU.not_equal, fill=1.0, base=0, channel_multiplier=1),
        reads=[id32], writes=[id32])
    S.c("dve", lambda h: h.tensor_copy(ident.ap, id32.ap), reads=[id32], writes=[ident])
    S.dma("sp", gain.ap, gvec.ap, reads=[gvec], writes=[gain])

    if "A" in st:
        x = din("x", [NTOK, D])
        meta = din("meta", [NMETA, D])
        w1g = din("w1g", [D, FF]); w1u = din("w1u", [D, FF]); w1d = din("w1d", [FF, D])
        h1m = dscr("h1m", [NMETA, D], F32)
        u_sv = u_send.ap.rearrange("(c p) t -> p c t", p=128)
        u_mv = u_meta.ap.rearrange("(c p) t -> p c t", p=128)

        def post_real(subs, xnT, xt_r, xs_r, small, junk, ctr):
            norm_transpose(K, h1, subs, gain, 16, xnT, ident, xt_r, xs_r, small, junk, ctr)
            r0 = subs[0][0]
            NT = sum(s[1] for s in subs)
            xT = v3(xnT.ap, KC)
            S.dma("sp", u_sv[:, :, r0:r0 + NT], xT[:, :, 0:NT], reads=[xnT], writes=[u_send])

        def post_meta(subs, xnT, xt_r, xs_r, small, junk, ctr):
            norm_transpose(K, h1m, subs, gain, 16, xnT, ident, xt_r, xs_r, small, junk, ctr)
            xT = v3(xnT.ap, KC)
            S.dma("sp", u_mv, xT[:, :, 0:NMETA], reads=[xnT], writes=[u_meta])

        real_tiles = [[(t0 + s * 128, 128) for s in range(8)] for t0 in range(0, NTOK, 1024)]
        ffn(K, meta, h1m, [[(0, NMETA)]], w1g, w1u, w1d, gain, 0, ident, post_meta)
        def a_setup(env):
            return K.ring("pstage", 2, KC * 128 * 2, BF16)

        def a_post(sub, env):
            r0, n, toff = sub
            stg = env["post"][env["ctr"][0] % 2]
            norm_transpose(K, h1, [(r0, n, 0)], gain, 16, stg, ident, env["xt_r"], env["xs_r"], env["small"], None, env["ctr"])
            S.dma("sp", u_sv[:, :, r0:r0 + n], v3(stg.ap, KC)[:, :, 0:n], reads=[stg], writes=[u_send])

        ffn2(K, x, h1, real_tiles, w1g, w1u, w1d, gain, 0, ident, a_setup, a_post)
        for i in range(4):
            S.coll(lambda h, i=i: h.collective_compute("AllGather", ALU.bypass, replica_groups=GROUPS,
                                                      ins=[u_send.ap[i * 512:(i + 1) * 512, :].opt()], outs=[u_all.ap[i * 1024:(i + 1) * 1024, :].opt()]),
                   reads=[u_send], writes=[u_all])
        S.barrier()

    if "B1" in st:
        win_fm = din("win_fm", [D, NFM * 128]); win_lr = din("win_lr", [D, 32]); win_tm = din("win_tm", [D, TMW])
        wa2 = din("wa2", [16, 1024]); ba = din("ba", [1, 1024])
        stage_b1(K, u_all, u_meta, win_fm, win_lr, win_tm, wa2, ba, pfm, ptm, Lg)

    def ag_y(i0, i1):
        for i in range(i0, i1):
            S.coll(lambda h, i=i: h.collective_compute("AllGather", ALU.bypass, replica_groups=GROUPS,
                                                      ins=[y_send.ap[i].opt()], outs=[y_all.ap[i].opt()]),
                   reads=[y_send], writes=[y_all])

    if "B2" in st:
        gmask = din("gmask", [128, 768]); gnw = din("gnw", [128, 512])
        stage_b2(K, pfm, ptm, Lg, gmask, gnw, y_send, ident)
    if "B3" in st:
        ttab = din("ttab", [128, TW]); slopes = din("slopes", [128, 4]); lamv = din("lamv", [128, 512]); dnw = din("dnw", [128, 256])
        stage_b3(K, pfm, ptm, ttab, slopes, lamv, dnw, y_send, ident)
        if not fB:
            ag_y(0, 8)
            S.barrier()

    if "C" in st:
        wgate = din("wgate", [D, 2 * D]); wA = din("wA", [D, D]); wB = din("wB", [D, D]); wO = din("wO", [D, D])
        w2g = din("w2g", [D, FF]); w2u = din("w2u", [D, FF]); w2d = din("w2d", [FF, D])
        gfin = din("gfin", [128, D])
        stage_c(K, u_send, y_all, h1, wgate, wA, wB, wO, h2, False)

        def post_final(subs, xnT, xt_r, xs_r, small_rr, junk, ctr):
            gf = xnT.ap.bitcast(F32)[:, 0:D]
            S.dma("sp", gf, gfin.ap, reads=[gfin], writes=[xnT])
            for (r0, n, toff) in subs:
                xt = xt_r[ctr[0] % 2]
                xs = xs_r[ctr[0] % 2]
                col = ctr[0] % 8
                ctr[0] += 1
                S.dma("sp", xt.ap[:n, :], h3.ap[r0:r0 + n, :], reads=[h3], writes=[xt])
                small = small_rr[col]
                ss = small.ap[:n, 0:1]
                rs = small.ap[:n, 1:2]
                S.c("dve", lambda h, ss=ss: h.memset(ss, 0.0), writes=[small])
                S.c("act", lambda h, xt=xt, xs=xs, n=n, ss=ss: h.activation(xs.ap[:n, :], xt.ap[:n, :], AF.Square, accum_out=ss), reads=[xt, small], writes=[xs, small])
                S.c("dve", lambda h, ss=ss, rs=rs: h.tensor_scalar(rs, ss, 1.0 / D, EPS, ALU.mult, ALU.add), reads=[small], writes=[small])
                S.c("act", lambda h, rs=rs: h.activation(rs, rs, AF.Sqrt), reads=[small], writes=[small])
                S.c("dve", lambda h, rs=rs: h.reciprocal(rs, rs), reads=[small], writes=[small])
                S.c("dve", lambda h, xt=xt, n=n, rs=rs, gf=gf: h.scalar_tensor_tensor(out=xt.ap[:n, :], in0=xt.ap[:n, :], scalar=rs, in1=gf[:n, :], op0=ALU.mult, op1=ALU.mult),
                    reads=[xt, small, xnT], writes=[xt])
                S.dma("sp", out.ap[r0:r0 + n, :], xt.ap[:n, :], reads=[xt], writes=[out])

        real_tiles = [[(t0 + s * 128, 128) for s in range(8)] for t0 in range(0, NTOK, 1024)]
        def c_setup(env):
            gft = K.alloc("gf", D * 4)
            S.dma("sp", gft.ap, gfin.ap, reads=[gfin], writes=[gft])
            return gft

        def c_post(sub, env):
            r0, n, toff = sub
            gft = env["post"]
            ctr = env["ctr"]
            xt = env["xt_r"][ctr[0] % 2]
            xs = env["xs_r"][ctr[0] % 2]
            small = env["small"][ctr[0] % 8]
            ctr[0] += 1
            S.dma("sp", xt.ap[:n, :], h3.ap[r0:r0 + n, :], reads=[h3], writes=[xt])
            ss = small.ap[:n, 0:1]
            rs = small.ap[:n, 1:2]
            S.c("dve", lambda h: h.memset(ss, 0.0), writes=[small])
            S.c("act", lambda h: h.activation(xs.ap[:n, :], xt.ap[:n, :], AF.Square, accum_out=ss), reads=[xt, small], writes=[xs, small])
            S.c("dve", lambda h: h.tensor_scalar(rs, ss, 1.0 / D, EPS, ALU.mult, ALU.add), reads=[small], writes=[small])
            S.c("act", lambda h: h.activation(rs, rs, AF.Sqrt), reads=[small], writes=[small])
            S.c("dve", lambda h: h.reciprocal(rs, rs), reads=[small], writes=[small])
            S.c("dve", lambda h: h.scalar_tensor_tensor(out=xt.ap[:n, :], in0=xt.ap[:n, :], scalar=rs, in1=gft.ap[:n, :], op0=ALU.mult, op1=ALU.mult),
                reads=[xt, small, gft], writes=[xt])
            S.dma("act", out.ap[r0:r0 + n, :], xt.ap[:n, :], reads=[xt], writes=[out])

        ffn2(K, h2, h3, real_tiles, w2g, w2u, w2d, gain, 32, ident, c_setup, c_post)

    table = {"h1": h1, "u_all": u_all, "u_meta": u_meta, "pfm": pfm, "ptm": ptm, "Lg": Lg, "y_send": y_send, "y_all": y_all, "h2": h2, "h3": h3, "u_send": u_send}
    for spec in dbg:
        parts = spec.split(":")
        name = parts[0]
        t = table[name]
        do = T("dbg_" + name, nc.dram_tensor("dbg_" + name, list(t.ap.shape), t.ap.dtype, kind="ExternalOutput").ap(), dram=True)
        if len(parts) == 4:
            a, b = int(parts[1]), int(parts[2])
            for i_ in range(t.ap.shape[0]):
                S.dma("sp", do.ap[i_, a:b], t.ap[i_, a:b], reads=[t], writes=[do])
        elif len(parts) == 3:
            a, b = int(parts[1]), int(parts[2])
            S.dma("sp", do.ap[a:b], t.ap[a:b], reads=[t], writes=[do])
        else:
            S.dma("sp", do.ap, t.ap, reads=[t], writes=[do])
    S.barrier()
    S.emit()
    return nc


def const_inputs():
    i = np.arange(128)[:, None]
    c = np.arange(TW)[None, :]
    ttab = -np.abs(c - C0 - i).astype(np.float32)
    s = np.arange(128)[:, None]
    t = np.arange(128)[None, :]
    same = (s // 64) == (t // 64)
    a = -1.0 / 16.0
    gm = np.concatenate([
        np.where(same & (s <= t), a, 0.0), np.where(same & (s > t), a, 0.0),
        np.where(same & (s >= t), a, 0.0), np.where(same & (s < t), a, 0.0),
        np.where(same & (s <= t), 1.0, 0.0), np.where(same & (s > t), 1.0, 0.0)], axis=1).astype(np.float32)
    return ttab, gm


GQ, GK, GV, GR, AFO, ABO, DQ, DK, DV, GA = 0, 1024, 2048, 4096, 6144, 6160, 6176, 8224, 10272, 12320


def make_inputs(inputs, pid, consts=None):
    b, j = pid // 2, pid % 2
    f = lambda a: np.ascontiguousarray(a, dtype=np.float32)
    ttab, gm = consts if consts is not None else const_inputs()
    gv = np.stack([inputs["ffn1_norm"][0], inputs["mix_norm"][0], inputs["ffn2_norm"][0], inputs["final_norm"]], 0)
    gvec = gv.reshape(4, 16, 128).transpose(2, 0, 1).reshape(128, 64)
    win = inputs["w_in"][0]
    cols = []
    for gl in range(2):
        g = 2 * j + gl
        cols += [np.arange(GQ + g * 256, GQ + (g + 1) * 256), np.arange(GK + g * 256, GK + (g + 1) * 256)]
    for dl in range(4):
        d = 4 * j + dl
        for m in range(2):
            cols += [np.arange(DQ + d * 256 + m * 128, DQ + d * 256 + (m + 1) * 128), np.arange(DK + d * 256 + m * 128, DK + d * 256 + (m + 1) * 128)]
    fm_cols = np.concatenate(cols)
    g0 = 2 * j
    tm_cols = np.concatenate([np.arange(GK + g0 * 256, GK + (g0 + 2) * 256), np.arange(GV + g0 * 512, GV + (g0 + 2) * 512),
                              np.arange(GR + g0 * 512, GR + (g0 + 2) * 512), np.arange(DV + 4 * j * 256, DV + (4 * j + 4) * 256)])
    wa2 = np.concatenate([inputs["gla_wa2_fwd"][0][:, g0 * 256:(g0 + 2) * 256], inputs["gla_wa2_bwd"][0][:, g0 * 256:(g0 + 2) * 256]], axis=1)
    ba = np.concatenate([inputs["gla_ba_fwd"][0][g0 * 256:(g0 + 2) * 256], inputs["gla_ba_bwd"][0][g0 * 256:(g0 + 2) * 256]])[None, :]
    slopes = np.array([2.0 ** (-8.0 * (4 * j + dl + 1) / 8) for dl in range(4)], np.float32)
    lamv = np.concatenate([inputs["diff_lambda_q1"][0], inputs["diff_lambda_k1"][0], inputs["diff_lambda_q2"][0], inputs["diff_lambda_k2"][0]])
    bc = lambda v: f(np.broadcast_to(np.asarray(v, np.float32)[None, :], (128, len(v))))
    m = {
        "x": f(inputs["x"][b, j * NTOK:(j + 1) * NTOK]),
        "meta": f(inputs["meta_tokens"]),
        "gvec": f(gvec),
        "w1g": f(inputs["ffn1_w_gate"][0]), "w1u": f(inputs["ffn1_w_up"][0]), "w1d": f(inputs["ffn1_w_down"][0]),
        "win_fm": f(win[:, fm_cols]), "win_lr": f(win[:, AFO:AFO + 32]), "win_tm": f(win[:, tm_cols]),
        "wa2": f(wa2), "ba": f(ba),
        "gmask": gm, "gnw": bc(inputs["gla_out_norm"][0]),
        "ttab": ttab, "slopes": bc(slopes), "lamv": bc(lamv), "dnw": bc(inputs["diff_out_norm"][0]),
        "wgate": f(win[:, GA:GA + 4096]), "wA": f(inputs["w_branch_gla"][0]), "wB": f(inputs["w_branch_diff"][0]), "wO": f(inputs["w_out"][0]),
        "w2g": f(inputs["ffn2_w_gate"][0]), "w2u": f(inputs["ffn2_w_up"][0]), "w2d": f(inputs["ffn2_w_down"][0]),
        "gfin": bc(inputs["final_norm"]),
    }
    return m


def kernel(**inputs):
    inputs = {k: np.asarray(v) for k, v in inputs.items()}
    nc = build()
    consts = const_inputs()
    in_maps = [make_inputs(inputs, pid, consts) for pid in range(8)]
    res = run_bass_kernel_spmd(nc, in_maps, core_ids=list(range(8)))
    outp = np.zeros((4, 4096, D), np.float32)
    for pid in range(8):
        b, j = pid // 2, pid % 2
        outp[b, j * NTOK:(j + 1) * NTOK] = res.results[pid]["out"]
    return outp
```

```python
import contextlib
import numpy as np
import ml_dtypes
import concourse.bass as bass
import concourse.mybir as mybir
from concourse.bass_utils import run_bass_kernel_spmd

F32 = mybir.dt.float32
BF16 = mybir.dt.bfloat16
AF = mybir.ActivationFunctionType
ALU = mybir.AluOpType
AX = mybir.AxisListType

ENGS = ["pe", "act", "dve", "pool", "sp"]
_PAR = {}

D = 2048
FF = 5504
NFC = 43
KC = 16
NTOK = 2048
NMETA = 16
LTOT = 4112
EPS = 1e-6
C0 = 3968
TW = 8080
GROUPS = [[0, 1], [2, 3], [4, 5], [6, 7]]


class T:
    __slots__ = ("name", "writer", "readers", "dram", "wd", "ap")

    def __init__(self, name, ap=None, dram=False):
        self.name = name
        self.writer = None
        self.readers = []
        self.dram = dram
        self.wd = []
        self.ap = ap


class Op:
    __slots__ = ("eng", "fn", "deps", "kind", "idx", "needed", "dsem", "dval", "seq")

    def __init__(self, eng, fn, kind):
        self.eng = eng
        self.fn = fn
        self.kind = kind
        self.deps = []
        self.idx = 0
        self.needed = False
        self.dsem = None
        self.dval = 0


class Sched:
    def __init__(self, nc):
        self.nc = nc
        self.ops = {e: [] for e in ENGS}
        self.lastc = {e: None for e in ENGS}
        self.pending_d = []
        self.nseq = 0

    def _rec(self, o, reads, writes):
        deps = []
        rw = set()
        for t in reads:
            if t.dram:
                deps.extend(t.wd)
            elif t.writer is not None:
                deps.append(t.writer)
                rw.add(id(t.writer))
        for t in writes:
            if t.dram:
                continue
            if t.writer is not None:
                deps.append(t.writer)
            deps.extend(t.readers)
        out = []
        seen = set()
        for d in deps:
            if id(d) in seen:
                continue
            seen.add(id(d))
            if d.kind == "c" and d.eng == o.eng and o.kind == "c":
                if o.eng == "pe":
                    continue
            out.append(d)
            d.needed = True
        o.deps = out
        for t in reads:
            if not t.dram:
                t.readers.append(o)
        for t in writes:
            if t.dram:
                t.wd.append(o)
            else:
                t.writer = o
                t.readers = []
        self.ops[o.eng].append(o)
        o.seq = self.nseq
        self.nseq += 1
        if o.kind == "c":
            self.lastc[o.eng] = o
        else:
            self.pending_d.append(o)
        return o

    def c(self, eng, fn, reads=(), writes=()):
        return self._rec(Op(eng, fn, "c"), reads, writes)

    def dma(self, q, out_ap, in_ap, reads=(), writes=()):
        return self._rec(Op(q, lambda h: h.dma_start(out=out_ap, in_=in_ap), "d"), reads, writes)

    def dmaf(self, q, fn, reads=(), writes=()):
        return self._rec(Op(q, fn, "d"), reads, writes)

    def coll(self, fn, reads=(), writes=()):
        return self._rec(Op("pool", fn, "x"), reads, writes)

    def barrier(self):
        lasts = [self.lastc[e] for e in ENGS if self.lastc[e] is not None]
        pend = list(self.pending_d)
        self.pending_d = []
        for e in ENGS:
            o = Op(e, None, "n")
            o.deps = [d for d in lasts if d.eng != e] + pend
            for d in o.deps:
                d.needed = True
            self.ops[e].append(o)

    def emit(self, nd=72):
        nc = self.nc
        for e in ENGS:
            i = 0
            for o in self.ops[e]:
                if o.kind == "c" and o.needed:
                    i += 1
                    o.idx = i
        cnt = [0] * nd
        k = 0
        nx = 0
        alld = sorted([o for e in ENGS for o in self.ops[e] if o.kind in "dx"], key=lambda o: o.seq)
        nsw = 28
        ksw = 0
        for _ in range(1):
            for o in alld:
                if o.kind == "d":
                    if o.eng == "pool":
                        s = ksw % nsw
                        ksw += 1
                    else:
                        s = nsw + k % (nd - nsw)
                        k += 1
                    cnt[s] += 16
                    o.dsem = s
                    o.dval = cnt[s]
                elif o.kind == "x":
                    o.dsem = nx
                    o.dval = 1
                    nx += 1
        with contextlib.ExitStack() as st:
            esem = {e: st.enter_context(nc.semaphore("s_" + e)) for e in ENGS}
            dsem = [st.enter_context(nc.semaphore("d%d" % i)) for i in range(nd)]
            xsem = [st.enter_context(nc.semaphore("x%d" % i)) for i in range(nx)]
            block = st.enter_context(nc.Block())

            def run(e, h):
                seen = {}
                for o in self.ops[e]:
                    w = {}
                    for d in o.deps:
                        if d.kind == "c":
                            key = ("e", d.eng)
                            v = d.idx
                        elif d.kind == "d":
                            key = ("d", d.dsem)
                            v = d.dval
                        else:
                            key = ("x", d.dsem)
                            v = 1
                        if w.get(key, 0) < v:
                            w[key] = v
                    if o.kind == "d" and o.dval > 16:
                        key = ("d", o.dsem)
                        if w.get(key, 0) < o.dval - 16:
                            w[key] = o.dval - 16
                    for key, v in w.items():
                        if seen.get(key, 0) >= v:
                            continue
                        seen[key] = v
                        sem = esem[key[1]] if key[0] == "e" else (dsem[key[1]] if key[0] == "d" else xsem[key[1]])
                        h.wait_ge(sem, v)
                    if o.kind == "n":
                        continue
                    ins = o.fn(h)
                    if o.kind == "c":
                        if o.needed:
                            ins.then_inc(esem[e], 1)
                    elif o.kind == "d":
                        ins.then_inc(dsem[o.dsem], 16)
                    else:
                        ins.then_inc(xsem[o.dsem], 1)

            @block.tensor
            def _(h):
                run("pe", h)

            @block.scalar
            def _(h):
                run("act", h)

            @block.vector
            def _(h):
                run("dve", h)

            @block.gpsimd
            def _(h):
                run("pool", h)

            @block.sync
            def _(h):
                par = h.partition_id() % 2
                _PAR["p"] = par
                for q_ in range(4):
                    _PAR[q_] = h.snap(par * 4 + q_)
                run("sp", h)


class KB:
    def __init__(self, nc, S):
        self.nc = nc
        self.S = S
        self.AW = 51200
        self.arena = nc.alloc_sbuf_tensor("arena", [128, self.AW], F32).ap()
        self.off = 0
        banks = [nc.alloc_psum_tensor("ps%d" % i, [128, 512], F32).ap() for i in range(8)]
        self.ps = [T("ps%d" % i, b) for i, b in enumerate(banks)]
        self.psc = {}
        self.uid = 0

    def alloc(self, name, nbytes, dt=F32):
        words = (nbytes + 3) // 4
        words = (words + 7) // 8 * 8
        assert self.off + words <= self.AW, (name, self.off, words)
        v = self.arena[:, self.off:self.off + words]
        self.off += words
        if dt != F32:
            v = v.bitcast(dt)[:, 0:nbytes // 2]
        else:
            v = v[:, 0:nbytes // 4]
        self.uid += 1
        return T("%s_%d" % (name, self.uid), v)

    def ring(self, name, n, nbytes, dt=F32):
        return [self.alloc(name + str(i), nbytes, dt) for i in range(n)]

    def nextps(self, lo=0, hi=8):
        k = self.psc.get((lo, hi), 0)
        self.psc[(lo, hi)] = k + 1
        return self.ps[lo + k % (hi - lo)]


def v3(ap, a):
    return ap.rearrange("p (a b) -> p a b", a=a)


def norm_transpose(K, src, subs, gain, gcol, dstT, ident, xt_ring, xs_ring, small_r, junk, ctr, psr=(0, 8)):
    S = K.S
    W = dstT.ap.shape[1] // KC
    dT = v3(dstT.ap, KC)
    for (r0, n, toff) in subs:
        xt = xt_ring[ctr[0] % len(xt_ring)]
        xs = xs_ring[ctr[0] % len(xs_ring)]
        col = ctr[0] % 8
        ctr[0] += 1
        S.dma("sp", xt.ap[:n, :], src.ap[r0:r0 + n, :], reads=[src], writes=[xt])
        small = small_r[col]
        ss = small.ap[:n, 0:1]
        rs = small.ap[:n, 1:2]
        S.c("dve", lambda h, ss=ss: h.memset(ss, 0.0), writes=[small])
        S.c("act", lambda h, xt=xt, xs=xs, n=n, ss=ss: h.activation(xs.ap[:n, :], xt.ap[:n, :], AF.Square, accum_out=ss),
            reads=[xt, small], writes=[xs, small])
        S.c("dve", lambda h, ss=ss, rs=rs: h.tensor_scalar(rs, ss, 1.0 / D, EPS, ALU.mult, ALU.add), reads=[small], writes=[small])
        S.c("act", lambda h, rs=rs: h.activation(rs, rs, AF.Sqrt), reads=[small], writes=[small])
        S.c("dve", lambda h, rs=rs: h.reciprocal(rs, rs), reads=[small], writes=[small])
        S.c("dve", lambda h, xs=xs, xt=xt, n=n, rs=rs: h.tensor_scalar(xs.ap[:n, :], xt.ap[:n, :], rs, None, ALU.mult),
            reads=[xt, small], writes=[xs])
        for half in range(2):
            ps = K.nextps(*psr)
            psb = ps.ap.bitcast(BF16)
            for cc in range(8):
                c = half * 8 + cc
                S.c("pe", lambda h, psb=psb, cc=cc, xs=xs, n=n, c=c: h.transpose(psb[:, cc * 128:cc * 128 + n], xs.ap[:n, c * 128:(c + 1) * 128], ident.ap[:n, :n]),
                    reads=[xs, ident], writes=[ps])
            pv = v3(psb, 8)
            gv = gain.ap[:, gcol + half * 8:gcol + half * 8 + 8].unsqueeze(2).to_broadcast([128, 8, n])
            S.c("dve", lambda h, dT=dT, half=half, toff=toff, n=n, pv=pv, gv=gv: h.tensor_tensor(dT[:, half * 8:half * 8 + 8, toff:toff + n], pv[:, :, :n], gv, ALU.mult),
                reads=[ps, gain], writes=[dstT])


def ffn(K, src, dst, tiles, Wg, Wu, Wd, gain, gcol, ident, post):
    S = K.S
    mark = K.off
    TT = 1024
    xnT = K.alloc("xnT", KC * TT * 2, BF16)
    actT = K.alloc("actT", NFC * TT * 2, BF16)
    wg_r = K.ring("wg", 3, KC * 128 * 2, BF16)
    wu_r = K.ring("wu", 3, KC * 128 * 2, BF16)
    wd_r = K.ring("wd", 3, 4 * 512 * 2, BF16)
    xt_r = K.ring("xt", 2, D * 4)
    xs_r = K.ring("xs", 2, D * 2, BF16)
    junk = None
    small = K.ring("small", 8, 32)
    sg_r = K.ring("sg", 3, 512 * 4)
    xb_r = K.ring("xb", 4, 512 * 4)
    hb_r = K.ring("hb", 2, 512 * 4)
    ctr = [0]
    wgv = Wg.ap.rearrange("(kc p) f -> p kc f", p=128)
    wuv = Wu.ap.rearrange("(kc p) f -> p kc f", p=128)
    wdv = Wd.ap.rearrange("(c p) d -> p c d", p=128)
    aT = v3(actT.ap, NFC)
    xT = v3(xnT.ap, KC)
    wi = 0
    di = 0
    ei = 0
    xi = [0]
    for subs0 in tiles:
        subs = []
        toff = 0
        for (r0, n) in subs0:
            subs.append((r0, n, toff))
            toff += n
        NT = toff
        norm_transpose(K, src, subs, gain, gcol, xnT, ident, xt_r, xs_r, small, junk, ctr)
        sts = [(s0, min(512, NT - s0)) for s0 in range(0, NT, 512)]
        for blk in range(NFC):
            wg = wg_r[wi % 3]
            wu = wu_r[wi % 3]
            wi += 1
            S.dma("pool", v3(wg.ap, KC), wgv[:, :, blk * 128:(blk + 1) * 128], reads=[Wg], writes=[wg])
            S.dma("pool", v3(wu.ap, KC), wuv[:, :, blk * 128:(blk + 1) * 128], reads=[Wu], writes=[wu])
            wg3 = v3(wg.ap, KC)
            wu3 = v3(wu.ap, KC)
            for (s0, nn) in sts:
                pg = K.nextps()
                pu = K.nextps()
                for kc in range(KC):
                    S.c("pe", lambda h, pg=pg, wg3=wg3, kc=kc, s0=s0, nn=nn: h.matmul(pg.ap[:, :nn], wg3[:, kc, :], xT[:, kc, s0:s0 + nn], start=(kc == 0), stop=(kc == KC - 1)),
                        reads=[wg, xnT], writes=[pg])
                for kc in range(KC):
                    S.c("pe", lambda h, pu=pu, wu3=wu3, kc=kc, s0=s0, nn=nn: h.matmul(pu.ap[:, :nn], wu3[:, kc, :], xT[:, kc, s0:s0 + nn], start=(kc == 0), stop=(kc == KC - 1)),
                        reads=[wu, xnT], writes=[pu])
                sg = sg_r[ei % 3]
                ei += 1
                S.c("act", lambda h, sg=sg, pg=pg, nn=nn: h.activation(sg.ap[:, :nn], pg.ap[:, :nn], AF.Silu), reads=[pg], writes=[sg])
                S.c("dve", lambda h, blk=blk, s0=s0, nn=nn, sg=sg, pu=pu: h.tensor_tensor(aT[:, blk, s0:s0 + nn], sg.ap[:, :nn], pu.ap[:, :nn], ALU.mult),
                    reads=[sg, pu], writes=[actT])
        for db in range(4):
            accs = [K.nextps() for _ in subs]
            for cg in range(0, NFC, 4):
                ncg = min(4, NFC - cg)
                wd = wd_r[di % 3]
                di += 1
                wd3 = v3(wd.ap, 4)
                S.dma("pool", wd3[:, :ncg, :], wdv[:, cg:cg + ncg, db * 512:(db + 1) * 512], reads=[Wd], writes=[wd])
                for i in range(ncg):
                    c = cg + i
                    for si, (r0, n, toff) in enumerate(subs):
                        S.c("pe", lambda h, acc=accs[si], c=c, toff=toff, n=n, wd3=wd3, i=i: h.matmul(acc.ap[:n, :], aT[:, c, toff:toff + n], wd3[:, i, :], start=(c == 0), stop=(c == NFC - 1)),
                            reads=[actT, wd], writes=[accs[si]])
            xbs = {}
            def ld(si):
                r0, n, toff = subs[si]
                xb = xb_r[(xi[0] + si) % 4]
                S.dma("sp", xb.ap[:n, :], src.ap[r0:r0 + n, db * 512:(db + 1) * 512], reads=[src], writes=[xb])
                xbs[si] = xb
            for si in range(min(3, len(subs))):
                ld(si)
            for si, (r0, n, toff) in enumerate(subs):
                if si + 3 < len(subs):
                    ld(si + 3)
                xb = xbs[si]
                hb = hb_r[ei % 2]
                ei += 1
                S.c("dve", lambda h, hb=hb, acc=accs[si], xb=xb, n=n: h.scalar_tensor_tensor(out=hb.ap[:n, :], in0=acc.ap[:n, :], scalar=0.5, in1=xb.ap[:n, :], op0=ALU.mult, op1=ALU.add),
                    reads=[accs[si], xb], writes=[hb])
                S.dma("act", dst.ap[r0:r0 + n, db * 512:(db + 1) * 512], hb.ap[:n, :], reads=[hb], writes=[dst])
            xi[0] += len(subs)
        post(subs, xnT, xt_r, xs_r, small, junk, ctr)
    S.barrier()
    K.off = mark


def ffn2(K, src, dst, tiles, Wg, Wu, Wd, gain, gcol, ident, post_setup, post_sub):
    S = K.S
    mark = K.off
    TT = 1024
    xnT = K.alloc("xnT", KC * TT * 2, BF16)
    actT = K.alloc("actT", NFC * TT * 2, BF16)
    wg_r = K.ring("wg", 3, KC * 128 * 2, BF16)
    wu_r = K.ring("wu", 3, KC * 128 * 2, BF16)
    wd_r = K.ring("wd", 3, 4 * 256 * 2, BF16)
    xt_r = K.ring("xt", 2, D * 4)
    xs_r = K.ring("xs", 2, D * 2, BF16)
    small = K.ring("small", 8, 32)
    sg_r = K.ring("sg", 3, 512 * 4)
    xb_r = K.ring("xb", 4, 256 * 4)
    hb_r = K.ring("hb", 2, 256 * 4)
    ctr = [0]
    env = dict(xt_r=xt_r, xs_r=xs_r, small=small, ctr=ctr)
    env["post"] = post_setup(env)
    wgv = Wg.ap.rearrange("(kc p) f -> p kc f", p=128)
    wuv = Wu.ap.rearrange("(kc p) f -> p kc f", p=128)
    wdv = Wd.ap.rearrange("(c p) d -> p c d", p=128)
    aT = v3(actT.ap, NFC)
    xT = v3(xnT.ap, KC)
    wi = 0
    di = 0
    ei = 0
    xi = 0

    def subs_of(t):
        out = []
        toff = 0
        for (r0, n) in t:
            out.append((r0, n, toff))
            toff += n
        return out

    norm_transpose(K, src, subs_of(tiles[0]), gain, gcol, xnT, ident, xt_r, xs_r, small, None, ctr)
    pending = []
    for ti, tile in enumerate(tiles):
        subs = subs_of(tile)
        NT = sum(x_[1] for x_ in subs)
        sts = [(s0, min(512, NT - s0)) for s0 in range(0, NT, 512)]
        for blk in range(NFC):
            wg = wg_r[wi % 3]
            wu = wu_r[wi % 3]
            wi += 1
            S.dma("pool", v3(wg.ap, KC), wgv[:, :, blk * 128:(blk + 1) * 128], reads=[Wg], writes=[wg])
            S.dma("pool", v3(wu.ap, KC), wuv[:, :, blk * 128:(blk + 1) * 128], reads=[Wu], writes=[wu])
            wg3 = v3(wg.ap, KC)
            wu3 = v3(wu.ap, KC)
            for (s0, nn) in sts:
                pg = K.nextps()
                pu = K.nextps()
                for kc in range(KC):
                    S.c("pe", lambda h, pg=pg, wg3=wg3, kc=kc, s0=s0, nn=nn: h.matmul(pg.ap[:, :nn], wg3[:, kc, :], xT[:, kc, s0:s0 + nn], start=(kc == 0), stop=(kc == KC - 1)),
                        reads=[wg, xnT], writes=[pg])
                for kc in range(KC):
                    S.c("pe", lambda h, pu=pu, wu3=wu3, kc=kc, s0=s0, nn=nn: h.matmul(pu.ap[:, :nn], wu3[:, kc, :], xT[:, kc, s0:s0 + nn], start=(kc == 0), stop=(kc == KC - 1)),
                        reads=[wu, xnT], writes=[pu])
                sg = sg_r[ei % 3]
                ei += 1
                S.c("act", lambda h, sg=sg, pg=pg, nn=nn: h.activation(sg.ap[:, :nn], pg.ap[:, :nn], AF.Silu), reads=[pg], writes=[sg])
                S.c("dve", lambda h, blk=blk, s0=s0, nn=nn, sg=sg, pu=pu: h.tensor_tensor(aT[:, blk, s0:s0 + nn], sg.ap[:, :nn], pu.ap[:, :nn], ALU.mult),
                    reads=[sg, pu], writes=[actT])
            if pending and blk % 5 == 4:
                post_sub(pending.pop(0), env)
        while pending:
            post_sub(pending.pop(0), env)
        nxt = subs_of(tiles[ti + 1]) if ti + 1 < len(tiles) else None
        for r in range(8):
            for cg in range(0, NFC, 4):
                ncg = min(4, NFC - cg)
                wd = wd_r[di % 3]
                di += 1
                wd3 = v3(wd.ap, 4)
                S.dma("pool", wd3[:, :ncg, :], wdv[:, cg:cg + ncg, r * 256:(r + 1) * 256], reads=[Wd], writes=[wd])
                for i in range(ncg):
                    c = cg + i
                    for si, (r0, n, toff) in enumerate(subs):
                        bank = K.ps[si // 2]
                        c0 = (si % 2) * 256
                        S.c("pe", lambda h, bank=bank, c0=c0, c=c, toff=toff, n=n, wd3=wd3, i=i, si=si: h.matmul(bank.ap[:n, c0:c0 + 256], aT[:, c, toff:toff + n], wd3[:, i, :], skip_group_check=True, start=(c == 0 and (si % 2 == 0)), stop=(c == NFC - 1 and (si % 2 == 1 or si == len(subs) - 1))),
                            reads=[actT, wd], writes=[bank])
            xbs = {}

            def ld(si, r=r, xbs=xbs):
                r0, n, toff = subs[si]
                xb = xb_r[(xi + si) % 4]
                S.dma("sp", xb.ap[:n, :], src.ap[r0:r0 + n, r * 256:(r + 1) * 256], reads=[src], writes=[xb])
                xbs[si] = xb
            for si in range(min(3, len(subs))):
                ld(si)
            for si, (r0, n, toff) in enumerate(subs):
                if si + 3 < len(subs):
                    ld(si + 3)
                xb = xbs[si]
                hb = hb_r[ei % 2]
                ei += 1
                bank = K.ps[si // 2]
                c0 = (si % 2) * 256
                S.c("dve", lambda h, hb=hb, bank=bank, c0=c0, xb=xb, n=n: h.scalar_tensor_tensor(out=hb.ap[:n, :], in0=bank.ap[:n, c0:c0 + 256], scalar=0.5, in1=xb.ap[:n, :], op0=ALU.mult, op1=ALU.add),
                    reads=[bank, xb], writes=[hb])
                S.dma("act", dst.ap[r0:r0 + n, r * 256:(r + 1) * 256], hb.ap[:n, :], reads=[hb], writes=[dst])
            xi += len(subs)
            if nxt is not None and r < len(nxt):
                norm_transpose(K, src, [nxt[r]], gain, gcol, xnT, ident, xt_r, xs_r, small, None, ctr, psr=(4, 8))
        pending = list(subs)
    while pending:
        post_sub(pending.pop(0), env)
    S.barrier()
    K.off = mark


NFM = 24
NTMB = 14
TMW = 3584
SUBS33 = [(s * 128, 128) for s in range(32)] + [(4096, 16)]
TT9 = [(t * 512, 512) for t in range(8)] + [(4096, 16)]


def load_uT(K, u_all, u_meta, uT):
    S = K.S
    u3 = v3(uT.ap, KC)
    ua = u_all.ap.rearrange("(i r q p) t -> r p i q t", i=4, r=2, q=4, p=128)
    for r in range(2):
        for i in range(4):
            S.dma("sp" if (r * 4 + i) % 2 == 0 else "act", u3[:, i * 4:(i + 1) * 4, r * 2048:(r + 1) * 2048], ua[r, :, i, :, :], reads=[u_all], writes=[uT])
    um = u_meta.ap.rearrange("(c p) t -> p c t", p=128)
    S.dma("sp", u3[:, :, 4096:4112], um, reads=[u_meta], writes=[uT])


def stage_b1(K, u_all, u_meta, win_fm, win_lr, win_tm, wa2, ba, pfm, ptm, Lg):
    S = K.S
    mark = K.off
    uT = K.alloc("uT", KC * LTOT * 2, BF16)
    u3 = v3(uT.ap, KC)
    load_uT(K, u_all, u_meta, uT)
    ei = [0]

    def evac(out_ap, in_ap, scale, reads, writes):
        ei[0] += 1
        if ei[0] % 2 == 0:
            S.c("act", lambda h: h.activation(out_ap, in_ap, AF.Copy, scale=float(scale)), reads=reads, writes=writes)
        else:
            S.c("dve", lambda h: h.tensor_scalar(out_ap, in_ap, float(scale), None, ALU.mult), reads=reads, writes=writes)

    m2 = K.off
    wlr = K.alloc("wlr", KC * 32 * 2, BF16)
    wlr3 = v3(wlr.ap, KC)
    S.dma("pool", wlr3, win_lr.ap.rearrange("(kc p) f -> p kc f", p=128), reads=[win_lr], writes=[wlr])
    wa2b = K.alloc("wa2b", 1024 * 2, BF16)
    bab = K.alloc("bab", 1024 * 2, BF16)
    ones = K.alloc("ones", 128 * 2, BF16)
    onef = K.alloc("onef", 4)
    S.dma("pool", wa2b.ap[:16, :], wa2.ap, reads=[wa2], writes=[wa2b])
    S.dma("pool", bab.ap[:1, :], ba.ap, reads=[ba], writes=[bab])
    S.c("dve", lambda h: h.memset(ones.ap, 1.0), writes=[ones])
    S.c("dve", lambda h: h.memset(onef.ap, 1.0), writes=[onef])
    plT = [K.alloc("plT", LTOT * 2, BF16) for _ in range(2)]
    ez_r = K.ring("ez", 2, 512 * 4)
    Lt_r = K.ring("Lt", 2, 512 * 4)
    for d_ in range(2):
        for (t0, nn) in TT9:
            ps = K.nextps()
            for kc in range(KC):
                S.c("pe", lambda h, ps=ps, kc=kc, d_=d_, t0=t0, nn=nn: h.matmul(ps.ap[:16, :nn], wlr3[:, kc, d_ * 16:(d_ + 1) * 16], u3[:, kc, t0:t0 + nn], start=(kc == 0), stop=(kc == KC - 1)),
                    reads=[wlr, uT], writes=[ps])
            evac(plT[d_].ap[:16, t0:t0 + nn], ps.ap[:16, :nn], 1.0, [ps], [plT[d_]])
    k = 0
    for d_ in range(2):
        for (t0, n) in SUBS33:
            ps = K.nextps()
            S.c("pe", lambda h, ps=ps, d_=d_, t0=t0, n=n: h.matmul(ps.ap[:n, :], plT[d_].ap[:16, t0:t0 + n], wa2b.ap[:16, d_ * 512:(d_ + 1) * 512], start=True, stop=False),
                reads=[plT[d_], wa2b], writes=[ps])
            S.c("pe", lambda h, ps=ps, d_=d_, n=n: h.matmul(ps.ap[:n, :], ones.ap[0:1, :n], bab.ap[0:1, d_ * 512:(d_ + 1) * 512], start=False, stop=True),
                reads=[ones, bab], writes=[ps])
            ez = ez_r[k % 2]
            Lt = Lt_r[k % 2]
            k += 1
            S.c("act", lambda h, ez=ez, ps=ps, n=n: h.activation(ez.ap[:n, :], ps.ap[:n, :], AF.Exp, scale=-1.0), reads=[ps], writes=[ez])
            S.c("act", lambda h, ez=ez, Lt=Lt, n=n: h.activation(Lt.ap[:n, :], ez.ap[:n, :], AF.Ln, bias=onef.ap[:n, 0:1]), reads=[ez, onef], writes=[Lt])
            S.dma("sp", Lg.ap[t0:t0 + n, d_ * 512:(d_ + 1) * 512], Lt.ap[:n, :], reads=[Lt], writes=[Lg])
    S.barrier()
    K.off = m2

    wb_r = K.ring("wb", 3, KC * 128 * 2, BF16)
    ofm_r = K.ring("ofm", 2, LTOT * 2, BF16)
    wfv = win_fm.ap.rearrange("(kc p) f -> p kc f", p=128)
    for ci in range(NFM):
        if ci < 8:
            sc = (1.0 / 16.0) if (ci % 4) < 2 else 1.0
        else:
            sc = (128.0 ** -0.5) if (ci % 2) == 0 else 1.0
        wb = wb_r[ci % 3]
        wb3 = v3(wb.ap, KC)
        S.dma("pool", wb3, wfv[:, :, ci * 128:(ci + 1) * 128], reads=[win_fm], writes=[wb])
        ofm = ofm_r[ci % 2]
        for (t0, nn) in TT9:
            ps = K.nextps()
            for kc in range(KC):
                S.c("pe", lambda h, ps=ps, kc=kc, wb3=wb3, t0=t0, nn=nn: h.matmul(ps.ap[:, :nn], wb3[:, kc, :], u3[:, kc, t0:t0 + nn], start=(kc == 0), stop=(kc == KC - 1)),
                    reads=[wb, uT], writes=[ps])
            evac(ofm.ap[:, t0:t0 + nn], ps.ap[:, :nn], sc, [ps], [ofm])
        S.dma("sp", pfm.ap[ci], ofm.ap, reads=[ofm], writes=[pfm])
    S.barrier()
    K.off = m2

    wt_r = K.ring("wt", 2, KC * 256 * 2, BF16)
    otm_r = K.ring("otm", 2, 33 * 256 * 2, BF16)
    wtv = win_tm.ap.rearrange("(kc p) f -> p kc f", p=128)
    ptr = ptm.ap[0:4096, :].rearrange("(s p) c -> p s c", p=128)
    for blk in range(NTMB):
        wt = wt_r[blk % 2]
        wt3 = v3(wt.ap, KC)
        S.dma("pool", wt3, wtv[:, :, blk * 256:(blk + 1) * 256], reads=[win_tm], writes=[wt])
        otm = otm_r[blk % 2]
        o3 = v3(otm.ap, 33)
        for si, (t0, n) in enumerate(SUBS33):
            ps = K.nextps()
            for kc in range(KC):
                S.c("pe", lambda h, ps=ps, kc=kc, wt3=wt3, t0=t0, n=n: h.matmul(ps.ap[:n, :256], u3[:, kc, t0:t0 + n], wt3[:, kc, :], start=(kc == 0), stop=(kc == KC - 1)),
                    reads=[wt, uT], writes=[ps])
            evac(o3[:n, si, :], ps.ap[:n, :256], 1.0, [ps], [otm])
        S.dma("sp", ptr[:, :, blk * 256:(blk + 1) * 256], o3[:, 0:32, :], reads=[otm], writes=[ptm])
        S.dma("sp", ptm.ap[4096:4112, blk * 256:(blk + 1) * 256], o3[:16, 32, :], reads=[otm], writes=[ptm])
    S.barrier()
    K.off = mark


def rms_rstd(K, small, col, ss, n, width):
    S = K.S
    rs = small.ap[:n, 16 + col:17 + col]
    S.c("dve", lambda h: h.tensor_scalar(rs, ss, 1.0 / width, EPS, ALU.mult, ALU.add), reads=[small], writes=[small])
    S.c("act", lambda h: h.activation(rs, rs, AF.Sqrt), reads=[small], writes=[small])
    S.c("dve", lambda h: h.reciprocal(rs, rs), reads=[small], writes=[small])
    return rs


def stage_b3(K, pfm, ptm, ttab, slopes_in, lamv, dnw_in, y_send, ident):
    S = K.S
    mark = K.off
    Tt = K.alloc("Tt", TW * 4)
    S.dma("sp", Tt.ap, ttab.ap, reads=[ttab], writes=[Tt])
    slp = K.alloc("slp", 16)
    S.dma("sp", slp.ap, slopes_in.ap, reads=[slopes_in], writes=[slp])
    lv = K.alloc("lv", 512 * 4)
    S.dma("sp", lv.ap, lamv.ap, reads=[lamv], writes=[lv])
    nw = K.alloc("nw", 256 * 4)
    S.dma("sp", nw.ap, dnw_in.ap, reads=[dnw_in], writes=[nw])
    S.c("dve", lambda h: h.tensor_scalar(nw.ap, nw.ap, 0.8, None, ALU.mult), reads=[nw], writes=[nw])
    small = K.alloc("small3", 64 * 4)
    pr = K.alloc("pr", 128 * 4)
    for i in range(2):
        S.c("dve", lambda h, i=i: h.tensor_tensor(pr.ap, lv.ap[:, i * 256:i * 256 + 128], lv.ap[:, i * 256 + 128:i * 256 + 256], ALU.mult), reads=[lv], writes=[pr])
        S.c("dve", lambda h, i=i: h.reduce_sum(small.ap[:, 1 + i:2 + i], pr.ap, AX.X), reads=[pr], writes=[small])
    S.c("act", lambda h: h.activation(small.ap[:, 1:3], small.ap[:, 1:3], AF.Exp), reads=[small], writes=[small])
    S.c("dve", lambda h: h.tensor_tensor(small.ap[:, 0:1], small.ap[:, 2:3], small.ap[:, 1:2], ALU.subtract), reads=[small], writes=[small])
    S.c("dve", lambda h: h.tensor_scalar(small.ap[:, 0:1], small.ap[:, 0:1], -0.2, None, ALU.add), reads=[small], writes=[small])
    neglam = small.ap[:, 0:1]

    kT = [K.alloc("kT", LTOT * 2, BF16) for _ in range(2)]
    qT = [K.alloc("qT", 4096 * 2, BF16) for _ in range(2)]
    vaug = K.alloc("vaug", 33 * 257 * 2, BF16)
    va3 = vaug.ap[:, 0:33 * 257].rearrange("p (a b) -> p a b", a=33)
    sb_r = K.ring("sb", 5, 512 * 4)
    pT_r = K.ring("pT", 5, 512 * 2, BF16)
    om = [K.alloc("om", 4 * 257 * 4) for _ in range(2)]
    a_t = K.alloc("a_t", 256 * 4)
    o_t = K.alloc("o_t", 256 * 4)
    jk = K.alloc("jk", 256 * 4)
    yb_r = K.ring("yb", 4, 256 * 2, BF16)
    yT = K.alloc("yTd", 2 * 4096 * 2, BF16)
    yT3 = v3(yT.ap, 2)
    ysv = y_send.ap[1].rearrange("i (c p) t -> i p c t", p=128)
    ptr = ptm.ap[0:4096, :].rearrange("(s p) c -> p s c", p=128)
    it = 0
    LA = 3
    for d in range(4):
        for m in range(2):
            S.dma("sp", kT[m].ap, pfm.ap[8 + d * 4 + m * 2 + 1], reads=[pfm], writes=[kT[m]])
            S.dma("act", qT[m].ap, pfm.ap[8 + d * 4 + m * 2][:, 0:4096], reads=[pfm], writes=[qT[m]])
        S.dma("sp", va3[:, 0:32, 0:256], ptr[:, :, 2560 + d * 256:2560 + (d + 1) * 256], reads=[ptm], writes=[vaug])
        S.dma("sp", va3[:16, 32, 0:256], ptm.ap[4096:4112, 2560 + d * 256:2560 + (d + 1) * 256], reads=[ptm], writes=[vaug])
        S.c("dve", lambda h: h.memset(va3[:, :, 256:257], 1.0), writes=[vaug])
        accs = [K.ps[a_] for a_ in range(4)]
        items = [(qt, m, kt) for qt in range(8) for m in range(2) for kt in range(33)]
        pts = {}
        deferred = []

        def emit_qk(idx):
            qt, m, kt = items[idx]
            k0, nk = SUBS33[kt]
            ps = K.nextps(4, 8)
            S.c("pe", lambda h, ps=ps, m=m, k0=k0, nk=nk, qt=qt: h.matmul(ps.ap[:nk, :], kT[m].ap[:, k0:k0 + nk], qT[m].ap[:, qt * 512:(qt + 1) * 512], start=True, stop=True),
                reads=[kT[m], qT[m]], writes=[ps])
            pk0 = (16 + k0) if kt < 32 else 0
            off = (16 + 512 * qt) - pk0 + C0
            sb = sb_r[idx % len(sb_r)]
            pT = pT_r[idx % len(pT_r)]
            S.c("dve", lambda h, sb=sb, ps=ps, nk=nk, off=off, d=d: h.scalar_tensor_tensor(out=sb.ap[:nk, :], in0=Tt.ap[:nk, off:off + 512], scalar=slp.ap[:nk, d:d + 1], in1=ps.ap[:nk, :], op0=ALU.mult, op1=ALU.add),
                reads=[Tt, slp, ps], writes=[sb])
            S.c("act", lambda h, sb=sb, pT=pT, nk=nk: h.activation(pT.ap[:nk, :], sb.ap[:nk, :], AF.Exp), reads=[sb], writes=[pT])
            pts[idx] = pT

        def emit_av(idx):
            qt, m, kt = items[idx]
            k0, nk = SUBS33[kt]
            pT = pts.pop(idx)
            for qs in range(4):
                S.c("pe", lambda h, acc=accs[qs], pT=pT, nk=nk, qs=qs, kt=kt: h.matmul(acc.ap[:, 0:257], pT.ap[:nk, qs * 128:(qs + 1) * 128], va3[:nk, kt, :], start=(kt == 0), stop=(kt == 32)),
                    reads=[pT, vaug], writes=[accs[qs]])
            if kt == 32:
                o3m = v3(om[m].ap, 4)
                for qs in range(4):
                    if qs % 2 == 0:
                        S.c("act", lambda h, o3m=o3m, qs=qs, acc=accs[qs]: h.activation(o3m[:, qs, :], acc.ap[:, 0:257], AF.Copy), reads=[accs[qs]], writes=[om[m]])
                    else:
                        S.c("dve", lambda h, o3m=o3m, qs=qs, acc=accs[qs]: h.tensor_copy(o3m[:, qs, :], acc.ap[:, 0:257]), reads=[accs[qs]], writes=[om[m]])
                if m == 1:
                    combine(qt, idx)

        def combine(qt, idx):
            o30 = v3(om[0].ap, 4)
            o31 = v3(om[1].ap, 4)
            for qs in range(4):
                S.c("dve", lambda h, qs=qs: h.reciprocal(small.ap[:, 4:5], o30[:, qs, 256:257]), reads=[om[0]], writes=[small])
                S.c("dve", lambda h, qs=qs: h.reciprocal(small.ap[:, 5:6], o31[:, qs, 256:257]), reads=[om[1]], writes=[small])
                S.c("dve", lambda h: h.tensor_tensor(small.ap[:, 5:6], small.ap[:, 5:6], neglam, ALU.mult), reads=[small], writes=[small])
                S.c("dve", lambda h, qs=qs: h.tensor_scalar(a_t.ap, o30[:, qs, 0:256], small.ap[:, 4:5], None, ALU.mult), reads=[om[0], small], writes=[a_t])
                S.c("dve", lambda h, qs=qs: h.scalar_tensor_tensor(out=o_t.ap, in0=o31[:, qs, 0:256], scalar=small.ap[:, 5:6], in1=a_t.ap, op0=ALU.mult, op1=ALU.add),
                    reads=[om[1], small, a_t], writes=[o_t])
                S.c("dve", lambda h: h.memset(small.ap[:, 6:7], 0.0), writes=[small])
                S.c("act", lambda h: h.activation(jk.ap, o_t.ap, AF.Square, accum_out=small.ap[:, 6:7]), reads=[o_t, small], writes=[jk, small])
                rs = rms_rstd(K, small, 0, small.ap[:, 6:7], 128, 256.0)
                yb = yb_r[qs]
                S.c("dve", lambda h, yb=yb, rs=rs: h.scalar_tensor_tensor(out=yb.ap, in0=o_t.ap, scalar=rs, in1=nw.ap, op0=ALU.mult, op1=ALU.mult),
                    reads=[o_t, small, nw], writes=[yb])

                def tr(yb=yb, qs=qs, qt=qt):
                    ps = K.nextps(4, 8)
                    psb = ps.ap.bitcast(BF16)
                    for fc in range(2):
                        S.c("pe", lambda h, psb=psb, fc=fc, yb=yb: h.transpose(psb[:, fc * 128:(fc + 1) * 128], yb.ap[:, fc * 128:(fc + 1) * 128], ident.ap), reads=[yb, ident], writes=[ps])
                    q0 = qt * 512 + qs * 128
                    S.c("act", lambda h, psb=psb, q0=q0: h.activation(yT3[:, :, q0:q0 + 128], v3(psb[:, 0:256], 2), AF.Copy), reads=[ps], writes=[yT])
                deferred.append((idx + 8 + 2 * qs, tr))

        n = len(items)
        for idx in range(n + LA):
            if idx < n:
                emit_qk(idx)
            if idx - LA >= 0:
                emit_av(idx - LA)
            while deferred and deferred[0][0] <= idx:
                deferred.pop(0)[1]()
        while deferred:
            deferred.pop(0)[1]()
        for i8 in range(8):
            S.dma("sp", ysv[i8][:, d * 2:2 + d * 2, :], yT3[:, :, i8 * 512:(i8 + 1) * 512], reads=[yT], writes=[y_send])
    S.barrier()
    K.off = mark


def stage_b2(K, pfm, ptm, Lg, gmask_in, gnw_in, y_send, ident):
    S = K.S
    mark = K.off
    gm = K.alloc("gm", 6 * 128 * 4)
    S.dma("sp", gm.ap, gmask_in.ap, reads=[gmask_in], writes=[gm])
    TRI = [gm.ap[:, 0:128], gm.ap[:, 256:384]]
    UPP = [gm.ap[:, 128:256], gm.ap[:, 384:512]]
    MSK = [gm.ap[:, 512:640], gm.ap[:, 640:768]]
    gnw = K.alloc("gnw", 512 * 4)
    S.dma("sp", gnw.ap, gnw_in.ap, reads=[gnw_in], writes=[gnw])
    small = K.alloc("small2", 64 * 4)
    qT = K.alloc("gqT", 2 * 4096 * 2, BF16); q3 = v3(qT.ap, 2)
    kT = K.alloc("gkT", 2 * LTOT * 2, BF16); k3 = v3(kT.ap, 2)
    ktm = K.alloc("gktm", 33 * 256 * 2, BF16); kt3 = v3(ktm.ap, 33)
    vtm = K.alloc("gvtm", 33 * 512 * 2, BF16); vt3 = v3(vtm.ap, 33)
    ofw = K.alloc("ofw", 32 * 512 * 2, BF16); of3 = v3(ofw.ap, 32)
    st_d = [[K.alloc("st", 512 * 4) for _ in range(2)] for _ in range(2)]
    stb_d = [[[K.alloc("stb", 512 * 2, BF16) for _ in range(2)] for _ in range(2)] for _ in range(2)]
    qdA_r = K.ring("qdA", 4, 256 * 2, BF16)
    qdB_r = K.ring("qdB", 4, 256 * 2, BF16)
    kiT_r = K.ring("kiT", 4, 256 * 2, BF16)
    kend_r = K.ring("kend", 4, 256 * 2, BF16)
    sm_r = K.ring("sm", 4, 128 * 2, BF16)
    Lt_r = K.ring("gLt", 4, 256 * 4)
    eb_r = K.ring("eb", 4, 256 * 4)
    enb_r = K.ring("enb", 4, 256 * 4)
    eE_r = K.ring("eE", 4, 256 * 4)
    rt_r = K.ring("rt", 2, 512 * 2, BF16)
    sr_r = K.ring("sr", 2, 512 * 4)
    of_r = K.ring("of", 2, 512 * 4)
    jk = K.alloc("gjk", 512 * 4)
    yb_r = K.ring("gyb", 2, 512 * 2, BF16)
    yt_r = K.ring("gyt", 2, 512 * 2, BF16)
    for t in qdA_r + qdB_r:
        S.c("dve", lambda h, t=t: h.memset(t.ap, 0.0), writes=[t])
    ysv = y_send.ap[0].rearrange("i (c p) t -> i p c t", p=128)
    ptr = ptm.ap[0:4096, :].rearrange("(s p) c -> p s c", p=128)
    cnt = [0]
    stcur = [0, 0]

    def state_update(g, kend, i, ch, nrows, eb3, col, dr):
        r0 = ch * 64
        stcur[dr] ^= 1
        pks = []
        for dc in range(2):
            pk = K.nextps(2, 8)
            S.c("pe", lambda h, pk=pk, dc=dc: h.matmul(pk.ap[:, :], kend.ap[r0:r0 + nrows, dc * 128:(dc + 1) * 128], vt3[r0:r0 + nrows, i, :], start=True, stop=True),
                reads=[kend, vtm], writes=[pk])
            pks.append(pk)
        for dc in range(2):
            pk = pks[dc]
            st = st_d[dr][dc]
            if eb3 is None:
                S.c("dve", lambda h, pk=pk, st=st: h.tensor_copy(st.ap, pk.ap[:, :]), reads=[pk], writes=[st])
            else:
                S.c("dve", lambda h, pk=pk, dc=dc, st=st: h.scalar_tensor_tensor(out=st.ap, in0=st.ap, scalar=eb3[0][:, dc, col:col + 1], in1=pk.ap[:, :], op0=ALU.mult, op1=ALU.add),
                    reads=[st, eb3[1], pk], writes=[st])
            nb = stb_d[dr][stcur[dr]][dc]
            S.c("act", lambda h, nb=nb, st=st: h.activation(nb.ap, st.ap, AF.Copy), reads=[st], writes=[nb])

    for g in range(2):
        for dc in range(2):
            S.dma("sp", q3[:, dc, :], pfm.ap[g * 4 + dc][:, 0:4096], reads=[pfm], writes=[qT])
            S.dma("act", k3[:, dc, :], pfm.ap[g * 4 + 2 + dc], reads=[pfm], writes=[kT])
        S.dma("sp", kt3[:, 0:32, :], ptr[:, :, g * 256:(g + 1) * 256], reads=[ptm], writes=[ktm])
        S.dma("sp", kt3[:16, 32, :], ptm.ap[4096:4112, g * 256:(g + 1) * 256], reads=[ptm], writes=[ktm])
        S.dma("act", vt3[:, 0:32, :], ptr[:, :, 512 + g * 512:512 + (g + 1) * 512], reads=[ptm], writes=[vtm])
        S.dma("act", vt3[:16, 32, :], ptm.ap[4096:4112, 512 + g * 512:512 + (g + 1) * 512], reads=[ptm], writes=[vtm])
        if True:
            if True:
                Lt = Lt_r[cnt[0] % 4]
                eE = eE_r[cnt[0] % 4]
                kend = kend_r[cnt[0] % 4]
                cnt[0] += 1
                S.dma("sp", Lt.ap[:16, :], Lg.ap[4096:4112, g * 256:(g + 1) * 256], reads=[Lg], writes=[Lt])
                pa = K.nextps(2, 8)
                S.c("pe", lambda h, pa=pa, Lt=Lt: h.matmul(pa.ap[:16, 0:256], UPP[0][:16, :16], Lt.ap[:16, :], start=True, stop=True), reads=[gm, Lt], writes=[pa])
                S.c("act", lambda h, pa=pa, eE=eE: h.activation(eE.ap[:16, :], pa.ap[:16, 0:256], AF.Exp), reads=[pa], writes=[eE])
                S.c("dve", lambda h, kend=kend, eE=eE: h.tensor_tensor(kend.ap[:16, :], kt3[:16, 32, :], eE.ap[:16, :], ALU.mult), reads=[ktm, eE], writes=[kend])
                state_update(g, kend, 32, 0, 16, None, 0, 0)
                stcur[1] ^= 1
                for dc_ in range(2):
                    S.c("dve", lambda h, dc_=dc_: h.memset(st_d[1][dc_].ap, 0.0), writes=[st_d[1][dc_]])
                    nb = stb_d[1][stcur[1]][dc_]
                    S.c("dve", lambda h, nb=nb: h.memset(nb.ap, 0.0), writes=[nb])
            def pre(i, dr, g=g):
                    t0 = i * 128
                    c_ = cnt[0]
                    cnt[0] += 1
                    Lt = Lt_r[c_ % 4]; eb = eb_r[c_ % 4]; enb = enb_r[c_ % 4]; eE = eE_r[c_ % 4]
                    qdA = qdA_r[c_ % 4]; qdB = qdB_r[c_ % 4]; kiT = kiT_r[c_ % 4]; kend = kend_r[c_ % 4]; sm = sm_r[c_ % 4]
                    S.dma("sp", Lt.ap, Lg.ap[t0:t0 + 128, dr * 512 + g * 256:dr * 512 + (g + 1) * 256], reads=[Lg], writes=[Lt])
                    pa = K.nextps(2, 8)
                    for dc in range(2):
                        S.c("pe", lambda h, pa=pa, dc=dc, Lt=Lt, dr=dr: h.matmul(pa.ap[:, dc * 128:(dc + 1) * 128], Lt.ap[:, dc * 128:(dc + 1) * 128], TRI[dr], start=True, stop=True),
                            reads=[Lt, gm], writes=[pa])
                    S.c("pe", lambda h, pa=pa, Lt=Lt, dr=dr: h.matmul(pa.ap[:, 256:512], UPP[dr], Lt.ap[:, :], start=True, stop=True), reads=[Lt, gm], writes=[pa])
                    S.c("act", lambda h, pa=pa, eb=eb: h.activation(eb.ap, pa.ap[:, 0:256], AF.Exp), reads=[pa], writes=[eb])
                    S.c("act", lambda h, pa=pa, enb=enb: h.activation(enb.ap, pa.ap[:, 0:256], AF.Exp, scale=-1.0), reads=[pa], writes=[enb])
                    S.c("act", lambda h, pa=pa, eE=eE: h.activation(eE.ap, pa.ap[:, 256:512], AF.Exp), reads=[pa], writes=[eE])
                    eb3 = v3(eb.ap, 2); enb3 = v3(enb.ap, 2)
                    qa3 = v3(qdA.ap, 2); qb3 = v3(qdB.ap, 2); ki3 = v3(kiT.ap, 2)
                    S.c("dve", lambda h, qa3=qa3, eb3=eb3, t0=t0: h.tensor_tensor(qa3[:, :, 0:64], q3[:, :, t0:t0 + 64], eb3[:, :, 0:64], ALU.mult), reads=[qT, eb], writes=[qdA])
                    S.c("dve", lambda h, qb3=qb3, eb3=eb3, t0=t0: h.tensor_tensor(qb3[:, :, 64:128], q3[:, :, t0 + 64:t0 + 128], eb3[:, :, 64:128], ALU.mult), reads=[qT, eb], writes=[qdB])
                    S.c("dve", lambda h, ki3=ki3, enb3=enb3, t0=t0: h.tensor_tensor(ki3, k3[:, :, t0:t0 + 128], enb3, ALU.mult), reads=[kT, enb], writes=[kiT])
                    S.c("dve", lambda h, kend=kend, eE=eE, i=i: h.tensor_tensor(kend.ap, kt3[:, i, :], eE.ap, ALU.mult), reads=[ktm, eE], writes=[kend])
                    psS = K.nextps(2, 8)
                    n_ = 0
                    for dc in range(2):
                        for (qd, q3_) in ((qdA, qa3), (qdB, qb3)):
                            S.c("pe", lambda h, psS=psS, ki3=ki3, dc=dc, q3_=q3_, n_=n_: h.matmul(psS.ap[:, 0:128], ki3[:, dc, :], q3_[:, dc, :], start=(n_ == 0), stop=(n_ == 3)),
                                reads=[kiT, qd], writes=[psS])
                            n_ += 1
                    S.c("dve", lambda h, sm=sm, psS=psS, dr=dr: h.tensor_tensor(sm.ap, psS.ap[:, 0:128], MSK[dr], ALU.mult), reads=[psS, gm], writes=[sm])

                    return dict(i=i, t0=t0, c_=c_, eb=eb, eb3=eb3, qdA=qdA, qdB=qdB, qa3=qa3, qb3=qb3, kend=kend, sm=sm)

            def chain(cx, dr, first, part, g=g):
                    i = cx["i"]; t0 = cx["t0"]; c_ = cx["c_"]; eb = cx["eb"]; eb3 = cx["eb3"]; qdA = cx["qdA"]; qdB = cx["qdB"]
                    qa3 = cx["qa3"]; qb3 = cx["qb3"]; kend = cx["kend"]; sm = cx["sm"]
                    chs = [0, 1] if dr == 0 else [1, 0]
                    if part == 0:
                        psO = K.ps[dr]
                        cx["psO"] = psO
                        S.c("pe", lambda h, psO=psO, sm=sm, i=i: h.matmul(psO.ap[:, :], sm.ap, vt3[:, i, :], start=True, stop=False), reads=[sm, vtm], writes=[psO])
                    psO = cx["psO"]
                    for ci_, ch in enumerate(chs):
                        if ci_ != part:
                            continue
                        qd, q3_ = (qdA, qa3) if ch == 0 else (qdB, qb3)
                        for dc in range(2):
                            sbt = stb_d[dr][stcur[dr]][dc]
                            S.c("pe", lambda h, psO=psO, q3_=q3_, dc=dc, sbt=sbt, last=(ci_ == 1 and dc == 1): h.matmul(psO.ap[:, :], q3_[:, dc, :], sbt.ap, start=False, stop=last),
                                reads=[qd, sbt], writes=[psO])
                        col = ch * 64 + (63 if dr == 0 else 0)
                        state_update(g, kend, i, ch, 64, (eb3, eb), col, dr)
                    if part == 0:
                        return
                    if first:
                        S.c("act", lambda h, psO=psO, i=i: h.activation(of3[:, i, :], psO.ap[:, :], AF.Copy), reads=[psO], writes=[ofw])
                    else:
                        of = of_r[c_ % 2]; rt = rt_r[c_ % 2]; sr = sr_r[c_ % 2]; yb = yb_r[c_ % 2]; yt = yt_r[c_ % 2]
                        S.dma("act", rt.ap, ptm.ap[t0:t0 + 128, 1536 + g * 512:1536 + (g + 1) * 512], reads=[ptm], writes=[rt])
                        S.c("dve", lambda h, of=of, psO=psO, i=i: h.tensor_tensor(of.ap, psO.ap[:, :], of3[:, i, :], ALU.add), reads=[psO, ofw], writes=[of])
                        S.c("dve", lambda h: h.memset(small.ap[:, 6:7], 0.0), writes=[small])
                        S.c("act", lambda h, of=of: h.activation(jk.ap, of.ap, AF.Square, accum_out=small.ap[:, 6:7]), reads=[of, small], writes=[jk, small])
                        rs = rms_rstd(K, small, 0, small.ap[:, 6:7], 128, 512.0)
                        S.c("act", lambda h, sr=sr, rt=rt: h.activation(sr.ap, rt.ap, AF.Silu), reads=[rt], writes=[sr])
                        S.c("dve", lambda h, of=of, rs=rs: h.scalar_tensor_tensor(out=of.ap, in0=of.ap, scalar=rs, in1=gnw.ap, op0=ALU.mult, op1=ALU.mult), reads=[of, small, gnw], writes=[of])
                        S.c("dve", lambda h, yb=yb, of=of, sr=sr: h.tensor_tensor(yb.ap, of.ap, sr.ap, ALU.mult), reads=[of, sr], writes=[yb])
                        pt = K.nextps(2, 8)
                        ptb = pt.ap.bitcast(BF16)
                        for fc in range(4):
                            S.c("pe", lambda h, ptb=ptb, fc=fc, yb=yb: h.transpose(ptb[:, fc * 128:(fc + 1) * 128], yb.ap[:, fc * 128:(fc + 1) * 128], ident.ap), reads=[yb, ident], writes=[pt])
                        S.c("act", lambda h, yt=yt, ptb=ptb: h.activation(yt.ap, ptb[:, 0:512], AF.Copy), reads=[pt], writes=[yt])
                        S.dma("sp", ysv[t0 // 512][:, g * 4:(g + 1) * 4, (t0 % 512):(t0 % 512) + 128], v3(yt.ap, 4), reads=[yt], writes=[y_send])

            seq = []
            for n_ in range(32):
                seq.append((0, n_, n_ < 16))
                seq.append((1, 31 - n_, n_ < 16))
            LAG = 2
            cxs = {}
            for k_ in range(LAG):
                cxs[k_] = pre(seq[k_][1], seq[k_][0])
            for k_ in range(0, len(seq), 2):
                for kk in (k_, k_ + 1):
                    if kk + LAG < len(seq):
                        cxs[kk + LAG] = pre(seq[kk + LAG][1], seq[kk + LAG][0])
                for part in range(2):
                    for kk in (k_, k_ + 1):
                        chain(cxs[kk], seq[kk][0], seq[kk][2], part)
                cxs.pop(k_); cxs.pop(k_ + 1)
    S.barrier()
    K.off = mark


def get_par(h):
    return _PAR["p"]


def stage_c(K, u_send, y_all, h1, wgate, wA, wB, wO, h2, fake_y):
    S = K.S
    mark = K.off
    TT = 1024
    uT = K.alloc("cuT", KC * TT * 2, BF16); u3 = v3(uT.ap, KC)
    mT = K.alloc("mT", KC * TT * 2, BF16); m3 = v3(mT.ap, KC)
    wr = [K.ring("cw%d" % i, 2, KC * 128 * 2, BF16) for i in range(4)]
    tmp_r = [K.ring("ctmp%d" % i, 2, 512 * 4) for i in range(4)]
    m_y = K.off
    yT = K.alloc("cyT", 32 * TT * 2, BF16); y3 = v3(yT.ap, 32)
    usv = u_send.ap.rearrange("(c p) t -> p c t", p=128)
    wgv = wgate.ap.rearrange("(kc p) f -> p kc f", p=128)
    wAv = wA.ap.rearrange("(kc p) f -> p kc f", p=128)
    wBv = wB.ap.rearrange("(kc p) f -> p kc f", p=128)
    wOv = wO.ap.rearrange("(kc p) f -> p kc f", p=128)
    wi = 0
    for tl in range(2):
        t0 = tl * TT
        S.dma("sp", u3, usv[:, :, t0:t0 + TT], reads=[u_send], writes=[uT])
        for r in range(2):
            for br in range(2):
                for ii in range(2):
                    S.dmaf("sp", lambda h, r=r, br=br, ii=ii, tl=tl: h.dma_start(out=y3[:, r * 16 + br * 8:r * 16 + br * 8 + 8, ii * 512:(ii + 1) * 512],
                                                                               in_=y_all.ap[br][bass.ds(_PAR[tl * 2 + ii], 1), r * 1024:(r + 1) * 1024, :].rearrange("o (c p) t -> p (o c) t", p=128)),
                           reads=[y_all], writes=[yT])
        for oc in range(KC):
            ws = [wr[i][wi % 2] for i in range(4)]
            wi += 1
            srcs = [wgv[:, :, oc * 128:(oc + 1) * 128], wgv[:, :, D + oc * 128:D + (oc + 1) * 128], wAv[:, :, oc * 128:(oc + 1) * 128], wBv[:, :, oc * 128:(oc + 1) * 128]]
            srcT = [wgate, wgate, wA, wB]
            for i in range(4):
                S.dma("pool", v3(ws[i].ap, KC), srcs[i], reads=[srcT[i]], writes=[ws[i]])
            w3 = [v3(w.ap, KC) for w in ws]
            for st_ in range(2):
                s0 = st_ * 512
                pss = [K.nextps() for _ in range(4)]
                for kc in range(KC):
                    S.c("pe", lambda h, kc=kc, s0=s0, p=pss[0], w=w3[0]: h.matmul(p.ap[:, :], w[:, kc, :], u3[:, kc, s0:s0 + 512], start=(kc == 0), stop=(kc == KC - 1)), reads=[ws[0], uT], writes=[pss[0]])
                for kc in range(KC):
                    S.c("pe", lambda h, kc=kc, s0=s0, p=pss[1], w=w3[1]: h.matmul(p.ap[:, :], w[:, kc, :], u3[:, kc, s0:s0 + 512], start=(kc == 0), stop=(kc == KC - 1)), reads=[ws[1], uT], writes=[pss[1]])
                for kc in range(KC):
                    yi = (kc // 8) * 16 + (kc % 8)
                    S.c("pe", lambda h, kc=kc, yi=yi, s0=s0, p=pss[2], w=w3[2]: h.matmul(p.ap[:, :], w[:, kc, :], y3[:, yi, s0:s0 + 512], start=(kc == 0), stop=(kc == KC - 1)), reads=[ws[2], yT], writes=[pss[2]])
                for kc in range(KC):
                    yi = (kc // 8) * 16 + 8 + (kc % 8)
                    S.c("pe", lambda h, kc=kc, yi=yi, s0=s0, p=pss[3], w=w3[3]: h.matmul(p.ap[:, :], w[:, kc, :], y3[:, yi, s0:s0 + 512], start=(kc == 0), stop=(kc == KC - 1)), reads=[ws[3], yT], writes=[pss[3]])
                tm = [tmp_r[i][(wi + st_) % 2] for i in range(4)]
                S.c("act", lambda h, t=tm[0], p=pss[0]: h.activation(t.ap, p.ap[:, :], AF.Sigmoid), reads=[pss[0]], writes=[tm[0]])
                S.c("act", lambda h, t=tm[1], p=pss[1]: h.activation(t.ap, p.ap[:, :], AF.Sigmoid), reads=[pss[1]], writes=[tm[1]])
                S.c("dve", lambda h, t=tm[2], a=tm[0], p=pss[2]: h.tensor_tensor(t.ap, a.ap, p.ap[:, :], ALU.mult), reads=[tm[0], pss[2]], writes=[tm[2]])
                S.c("dve", lambda h, t=tm[3], a=tm[1], p=pss[3]: h.tensor_tensor(t.ap, a.ap, p.ap[:, :], ALU.mult), reads=[tm[1], pss[3]], writes=[tm[3]])
                S.c("pool", lambda h, oc=oc, s0=s0, a=tm[2], b=tm[3]: h.tensor_tensor(m3[:, oc, s0:s0 + 512], a.ap, b.ap, ALU.add), reads=[tm[2], tm[3]], writes=[mT])
        S.barrier()
        keep = K.off
        K.off = m_y
        wo_r = K.ring("wo", 2, KC * 512 * 2, BF16)
        hb_r = K.ring("chb", 2, 512 * 4)
        ho_r = K.ring("cho", 2, 512 * 4)
        k = 0
        for db in range(4):
            wo = wo_r[db % 2]
            wo3 = v3(wo.ap, KC)
            S.dma("pool", wo3, wOv[:, :, db * 512:(db + 1) * 512], reads=[wO], writes=[wo])
            for s in range(8):
                r0 = t0 + s * 128
                ps = K.nextps()
                for kc in range(KC):
                    S.c("pe", lambda h, ps=ps, kc=kc, s=s, wo3=wo3: h.matmul(ps.ap[:, :], m3[:, kc, s * 128:(s + 1) * 128], wo3[:, kc, :], start=(kc == 0), stop=(kc == KC - 1)), reads=[mT, wo], writes=[ps])
                hb = hb_r[k % 2]; ho = ho_r[k % 2]
                k += 1
                S.dma("sp", hb.ap, h1.ap[r0:r0 + 128, db * 512:(db + 1) * 512], reads=[h1], writes=[hb])
                S.c("dve", lambda h, ho=ho, ps=ps, hb=hb: h.tensor_tensor(ho.ap, ps.ap[:, :], hb.ap, ALU.add), reads=[ps, hb], writes=[ho])
                S.dma("sp", h2.ap[r0:r0 + 128, db * 512:(db + 1) * 512], ho.ap, reads=[ho], writes=[h2])
        S.barrier()
        K.off = keep
    S.barrier()
    K.off = mark


ALL_STAGES = ("A", "B1", "B2", "B3", "C")


def build(stages=ALL_STAGES, dbg=()):
    _PAR.clear()
    nc = bass.Bass("TRN2", target_bir_lowering=False)
    S = Sched(nc)
    K = KB(nc, S)
    st = set(stages)

    def din(name, shape, dt=F32):
        return T(name, nc.dram_tensor(name, list(shape), dt, kind="ExternalInput").ap(), dram=True)

    def dscr(name, shape, dt, fake=False):
        if fake:
            return din(name, shape, dt)
        return T(name, nc.dram_tensor(name, list(shape), dt).ap(), dram=True)

    fA = "A" not in st
    fB1 = "B1" not in st
    fB = not ("B2" in st and "B3" in st)
    gvec = din("gvec", [128, 64])
    out = T("out", nc.dram_tensor("out", [NTOK, D], F32, kind="ExternalOutput").ap(), dram=True)
    h1 = dscr("h1", [NTOK, D], F32, fA)
    u_send = dscr("u_send", [D, NTOK], BF16, fA)
    u_all = dscr("u_all", [2 * D, NTOK], BF16, fA)
    u_meta = dscr("u_meta", [D, NMETA], BF16, fA)
    pfm = dscr("pfm", [NFM, 128, LTOT], BF16, fB1)
    ptm = dscr("ptm", [LTOT, TMW], BF16, fB1)
    Lg = dscr("Lg", [LTOT, 1024], F32, fB1)
    y_send = dscr("y_send", [2, 8, 1024, 512], BF16)
    y_all = dscr("y_all", [2, 8, 2048, 512], BF16, fB)
    h2 = dscr("h2", [NTOK, D], F32)
    h3 = dscr("h3", [NTOK, D], F32)

    ident = K.alloc("ident", 128 * 2, BF16)
    id32 = K.alloc("id32", 128 * 4)
    gain = K.alloc("gain", 64 * 4)
    S.c("pool", lambda h: h.memset(id32.ap, 0.0), writes=[id32])
    S.c("pool", lambda h: h.affine_select(out=id32.ap, in_=id32.ap, pattern=[[-1, 128]], compare_op=ALU.not_equal, fill=1.0, base=0, channel_multiplier=1),
        reads=[id32], writes=[id32])
    S.c("dve", lambda h: h.tensor_copy(ident.ap, id32.ap), reads=[id32], writes=[ident])
    S.dma("sp", gain.ap, gvec.ap, reads=[gvec], writes=[gain])

    if "A" in st:
        x = din("x", [NTOK, D])
        meta = din("meta", [NMETA, D])
        w1g = din("w1g", [D, FF]); w1u = din("w1u", [D, FF]); w1d = din("w1d", [FF, D])
        h1m = dscr("h1m", [NMETA, D], F32)
        u_sv = u_send.ap.rearrange("(c p) t -> p c t", p=128)
        u_mv = u_meta.ap.rearrange("(c p) t -> p c t", p=128)

        def post_real(subs, xnT, xt_r, xs_r, small, junk, ctr):
            norm_transpose(K, h1, subs, gain, 16, xnT, ident, xt_r, xs_r, small, junk, ctr)
            r0 = subs[0][0]
            NT = sum(s[1] for s in subs)
            xT = v3(xnT.ap, KC)
            S.dma("sp", u_sv[:, :, r0:r0 + NT], xT[:, :, 0:NT], reads=[xnT], writes=[u_send])

        def post_meta(subs, xnT, xt_r, xs_r, small, junk, ctr):
            norm_transpose(K, h1m, subs, gain, 16, xnT, ident, xt_r, xs_r, small, junk, ctr)
            xT = v3(xnT.ap, KC)
            S.dma("sp", u_mv, xT[:, :, 0:NMETA], reads=[xnT], writes=[u_meta])

        real_tiles = [[(t0 + s * 128, 128) for s in range(8)] for t0 in range(0, NTOK, 1024)]
        ffn(K, meta, h1m, [[(0, NMETA)]], w1g, w1u, w1d, gain, 0, ident, post_meta)
        def a_setup(env):
            return K.ring("pstage", 2, KC * 128 * 2, BF16)

        def a_post(sub, env):
            r0, n, toff = sub
            stg = env["post"][env["ctr"][0] % 2]
            norm_transpose(K, h1, [(r0, n, 0)], gain, 16, stg, ident, env["xt_r"], env["xs_r"], env["small"], None, env["ctr"])
            S.dma("sp", u_sv[:, :, r0:r0 + n], v3(stg.ap, KC)[:, :, 0:n], reads=[stg], writes=[u_send])

        ffn2(K, x, h1, real_tiles, w1g, w1u, w1d, gain, 0, ident, a_setup, a_post)
        for i in range(4):
            S.coll(lambda h, i=i: h.collective_compute("AllGather", ALU.bypass, replica_groups=GROUPS,
                                                      ins=[u_send.ap[i * 512:(i + 1) * 512, :].opt()], outs=[u_all.ap[i * 1024:(i + 1) * 1024, :].opt()]),
                   reads=[u_send], writes=[u_all])
        S.barrier()

    if "B1" in st:
        win_fm = din("win_fm", [D, NFM * 128]); win_lr = din("win_lr", [D, 32]); win_tm = din("win_tm", [D, TMW])
        wa2 = din("wa2", [16, 1024]); ba = din("ba", [1, 1024])
        stage_b1(K, u_all, u_meta, win_fm, win_lr, win_tm, wa2, ba, pfm, ptm, Lg)

    def ag_y(br):
        for i in range(8):
            S.coll(lambda h, i=i: h.collective_compute("AllGather", ALU.bypass, replica_groups=GROUPS,
                                                      ins=[y_send.ap[br, i].opt()], outs=[y_all.ap[br, i].opt()]),
                   reads=[y_send], writes=[y_all])

    if "B2" in st:
        gmask = din("gmask", [128, 768]); gnw = din("gnw", [128, 512])
        stage_b2(K, pfm, ptm, Lg, gmask, gnw, y_send, ident)
        if not fB:
            ag_y(0)
    if "B3" in st:
        ttab = din("ttab", [128, TW]); slopes = din("slopes", [128, 4]); lamv = din("lamv", [128, 512]); dnw = din("dnw", [128, 256])
        stage_b3(K, pfm, ptm, ttab, slopes, lamv, dnw, y_send, ident)
        if not fB:
            ag_y(1)

    if "C" in st:
        wgate = din("wgate", [D, 2 * D]); wA = din("wA", [D, D]); wB = din("wB", [D, D]); wO = din("wO", [D, D])
        w2g = din("w2g", [D, FF]); w2u = din("w2u", [D, FF]); w2d = din("w2d", [FF, D])
        gfin = din("gfin", [128, D])
        stage_c(K, u_send, y_all, h1, wgate, wA, wB, wO, h2, False)

        def post_final(subs, xnT, xt_r, xs_r, small_rr, junk, ctr):
            gf = xnT.ap.bitcast(F32)[:, 0:D]
            S.dma("sp", gf, gfin.ap, reads=[gfin], writes=[xnT])
            for (r0, n, toff) in subs:
                xt = xt_r[ctr[0] % 2]
                xs = xs_r[ctr[0] % 2]
                col = ctr[0] % 8
                ctr[0] += 1
                S.dma("sp", xt.ap[:n, :], h3.ap[r0:r0 + n, :], reads=[h3], writes=[xt])
                small = small_rr[col]
                ss = small.ap[:n, 0:1]
                rs = small.ap[:n, 1:2]
                S.c("dve", lambda h, ss=ss: h.memset(ss, 0.0), writes=[small])
                S.c("act", lambda h, xt=xt, xs=xs, n=n, ss=ss: h.activation(xs.ap[:n, :], xt.ap[:n, :], AF.Square, accum_out=ss), reads=[xt, small], writes=[xs, small])
                S.c("dve", lambda h, ss=ss, rs=rs: h.tensor_scalar(rs, ss, 1.0 / D, EPS, ALU.mult, ALU.add), reads=[small], writes=[small])
                S.c("act", lambda h, rs=rs: h.activation(rs, rs, AF.Sqrt), reads=[small], writes=[small])
                S.c("dve", lambda h, rs=rs: h.reciprocal(rs, rs), reads=[small], writes=[small])
                S.c("dve", lambda h, xt=xt, n=n, rs=rs, gf=gf: h.scalar_tensor_tensor(out=xt.ap[:n, :], in0=xt.ap[:n, :], scalar=rs, in1=gf[:n, :], op0=ALU.mult, op1=ALU.mult),
                    reads=[xt, small, xnT], writes=[xt])
                S.dma("sp", out.ap[r0:r0 + n, :], xt.ap[:n, :], reads=[xt], writes=[out])

        real_tiles = [[(t0 + s * 128, 128) for s in range(8)] for t0 in range(0, NTOK, 1024)]
        def c_setup(env):
            gft = K.alloc("gf", D * 4)
            S.dma("sp", gft.ap, gfin.ap, reads=[gfin], writes=[gft])
            return gft

        def c_post(sub, env):
            r0, n, toff = sub
            gft = env["post"]
            ctr = env["ctr"]
            xt = env["xt_r"][ctr[0] % 2]
            xs = env["xs_r"][ctr[0] % 2]
            small = env["small"][ctr[0] % 8]
            ctr[0] += 1
            S.dma("sp", xt.ap[:n, :], h3.ap[r0:r0 + n, :], reads=[h3], writes=[xt])
            ss = small.ap[:n, 0:1]
            rs = small.ap[:n, 1:2]
            S.c("dve", lambda h: h.memset(ss, 0.0), writes=[small])
            S.c("act", lambda h: h.activation(xs.ap[:n, :], xt.ap[:n, :], AF.Square, accum_out=ss), reads=[xt, small], writes=[xs, small])
            S.c("dve", lambda h: h.tensor_scalar(rs, ss, 1.0 / D, EPS, ALU.mult, ALU.add), reads=[small], writes=[small])
            S.c("act", lambda h: h.activation(rs, rs, AF.Sqrt), reads=[small], writes=[small])
            S.c("dve", lambda h: h.reciprocal(rs, rs), reads=[small], writes=[small])
            S.c("dve", lambda h: h.scalar_tensor_tensor(out=xt.ap[:n, :], in0=xt.ap[:n, :], scalar=rs, in1=gft.ap[:n, :], op0=ALU.mult, op1=ALU.mult),
                reads=[xt, small, gft], writes=[xt])
            S.dma("act", out.ap[r0:r0 + n, :], xt.ap[:n, :], reads=[xt], writes=[out])

        ffn2(K, h2, h3, real_tiles, w2g, w2u, w2d, gain, 32, ident, c_setup, c_post)

    table = {"h1": h1, "u_all": u_all, "u_meta": u_meta, "pfm": pfm, "ptm": ptm, "Lg": Lg, "y_send": y_send, "y_all": y_all, "h2": h2, "h3": h3, "u_send": u_send}
    for spec in dbg:
        parts = spec.split(":")
        name = parts[0]
        t = table[name]
        do = T("dbg_" + name, nc.dram_tensor("dbg_" + name, list(t.ap.shape), t.ap.dtype, kind="ExternalOutput").ap(), dram=True)
        if len(parts) == 4:
            a, b = int(parts[1]), int(parts[2])
            for i_ in range(t.ap.shape[0]):
                S.dma("sp", do.ap[i_, a:b], t.ap[i_, a:b], reads=[t], writes=[do])
        elif len(parts) == 3:
            a, b = int(parts[1]), int(parts[2])
            S.dma("sp", do.ap[a:b], t.ap[a:b], reads=[t], writes=[do])
        else:
            S.dma("sp", do.ap, t.ap, reads=[t], writes=[do])
    S.barrier()
    S.emit()
    return nc


def const_inputs():
    i = np.arange(128)[:, None]
    c = np.arange(TW)[None, :]
    ttab = -np.abs(c - C0 - i).astype(np.float32)
    s = np.arange(128)[:, None]
    t = np.arange(128)[None, :]
    same = (s // 64) == (t // 64)
    a = -1.0 / 16.0
    gm = np.concatenate([
        np.where(same & (s <= t), a, 0.0), np.where(same & (s > t), a, 0.0),
        np.where(same & (s >= t), a, 0.0), np.where(same & (s < t), a, 0.0),
        np.where(same & (s <= t), 1.0, 0.0), np.where(same & (s > t), 1.0, 0.0)], axis=1).astype(np.float32)
    return ttab, gm


GQ, GK, GV, GR, AFO, ABO, DQ, DK, DV, GA = 0, 1024, 2048, 4096, 6144, 6160, 6176, 8224, 10272, 12320


def make_inputs(inputs, pid, consts=None):
    b, j = pid // 2, pid % 2
    f = lambda a: np.ascontiguousarray(a, dtype=np.float32)
    ttab, gm = consts if consts is not None else const_inputs()
    gv = np.stack([inputs["ffn1_norm"][0], inputs["mix_norm"][0], inputs["ffn2_norm"][0], inputs["final_norm"]], 0)
    gvec = gv.reshape(4, 16, 128).transpose(2, 0, 1).reshape(128, 64)
    win = inputs["w_in"][0]
    cols = []
    for gl in range(2):
        g = 2 * j + gl
        cols += [np.arange(GQ + g * 256, GQ + (g + 1) * 256), np.arange(GK + g * 256, GK + (g + 1) * 256)]
    for dl in range(4):
        d = 4 * j + dl
        for m in range(2):
            cols += [np.arange(DQ + d * 256 + m * 128, DQ + d * 256 + (m + 1) * 128), np.arange(DK + d * 256 + m * 128, DK + d * 256 + (m + 1) * 128)]
    fm_cols = np.concatenate(cols)
    g0 = 2 * j
    tm_cols = np.concatenate([np.arange(GK + g0 * 256, GK + (g0 + 2) * 256), np.arange(GV + g0 * 512, GV + (g0 + 2) * 512),
                              np.arange(GR + g0 * 512, GR + (g0 + 2) * 512), np.arange(DV + 4 * j * 256, DV + (4 * j + 4) * 256)])
    wa2 = np.concatenate([inputs["gla_wa2_fwd"][0][:, g0 * 256:(g0 + 2) * 256], inputs["gla_wa2_bwd"][0][:, g0 * 256:(g0 + 2) * 256]], axis=1)
    ba = np.concatenate([inputs["gla_ba_fwd"][0][g0 * 256:(g0 + 2) * 256], inputs["gla_ba_bwd"][0][g0 * 256:(g0 + 2) * 256]])[None, :]
    slopes = np.array([2.0 ** (-8.0 * (4 * j + dl + 1) / 8) for dl in range(4)], np.float32)
    lamv = np.concatenate([inputs["diff_lambda_q1"][0], inputs["diff_lambda_k1"][0], inputs["diff_lambda_q2"][0], inputs["diff_lambda_k2"][0]])
    bc = lambda v: f(np.broadcast_to(np.asarray(v, np.float32)[None, :], (128, len(v))))
    m = {
        "x": f(inputs["x"][b, j * NTOK:(j + 1) * NTOK]),
        "meta": f(inputs["meta_tokens"]),
        "gvec": f(gvec),
        "w1g": f(inputs["ffn1_w_gate"][0]), "w1u": f(inputs["ffn1_w_up"][0]), "w1d": f(inputs["ffn1_w_down"][0]),
        "win_fm": f(win[:, fm_cols]), "win_lr": f(win[:, AFO:AFO + 32]), "win_tm": f(win[:, tm_cols]),
        "wa2": f(wa2), "ba": f(ba),
        "gmask": gm, "gnw": bc(inputs["gla_out_norm"][0]),
        "ttab": ttab, "slopes": bc(slopes), "lamv": bc(lamv), "dnw": bc(inputs["diff_out_norm"][0]),
        "wgate": f(win[:, GA:GA + 4096]), "wA": f(inputs["w_branch_gla"][0]), "wB": f(inputs["w_branch_diff"][0]), "wO": f(inputs["w_out"][0]),
        "w2g": f(inputs["ffn2_w_gate"][0]), "w2u": f(inputs["ffn2_w_up"][0]), "w2d": f(inputs["ffn2_w_down"][0]),
        "gfin": bc(inputs["final_norm"]),
    }
    return m


def kernel(**inputs):
    inputs = {k: np.asarray(v) for k, v in inputs.items()}
    nc = build()
    consts = const_inputs()
    in_maps = [make_inputs(inputs, pid, consts) for pid in range(8)]
    res = run_bass_kernel_spmd(nc, in_maps, core_ids=list(range(8)))
    outp = np.zeros((4, 4096, D), np.float32)
    for pid in range(8):
        b, j = pid // 2, pid % 2
        outp[b, j * NTOK:(j + 1) * NTOK] = res.results[pid]["out"]
    return outp
```
